# Optimizing a Trainium2 kernel written in Bass

```python
import math
import jax, jax.numpy as jnp
from jax import lax
import numpy as np

D_MODEL = 1024
BATCH = 8
SEQ = 4096
DEPTH = 2

CHUNK = 64
Q_BLOCK = 128
HEAD_DIM = 64
H_DIFF = 4
H_FOX = 6
H_CHUNK = 6
N_PREV_CHUNKS = 8
REL_CLIP = 128
D_FF = 2816
CONV_WIDTH = 3
PLE_DIM = 256
EPS = 1e-6
NEG_INF = -1e30

W_DIFF = H_DIFF * HEAD_DIM
W_FOX = H_FOX * HEAD_DIM
W_CHUNK = H_CHUNK * HEAD_DIM
MIX_WIDTH = W_DIFF + W_FOX + W_CHUNK
IN_SIZES = (W_DIFF,) * 5 + (W_FOX,) * 3 + (H_FOX,) + (W_CHUNK,) * 3
IN_COLS = sum(IN_SIZES)
IN_SPLITS = tuple(int(c) for c in np.cumsum(IN_SIZES)[:-1])
ALIBI_SLOPES = tuple(2.0 ** (-8.0 * (h + 1) / H_DIFF) for h in range(H_DIFF))

kernel_name = "hymba_style_chunk_causal_hybrid_block"


def rms_norm(x, g):
    xf = x.astype(jnp.float32)
    y = xf * lax.rsqrt(jnp.mean(xf * xf, axis=-1, keepdims=True) + EPS)
    return (y * g.astype(jnp.float32)).astype(x.dtype)


def split_heads(t, n_heads):
    return t.reshape(t.shape[0], t.shape[1], n_heads, HEAD_DIM)


def query_blocks(q):
    b, s = q.shape[0], q.shape[1]
    q = q.reshape((b, s // Q_BLOCK, Q_BLOCK) + q.shape[2:])
    return jnp.swapaxes(q, 0, 1)


def merge_blocks(o):
    nb, b, qb, h, d = o.shape
    return jnp.swapaxes(o, 0, 1).reshape(b, nb * qb, h * d)


def diff_attention(q1, q2, k1, k2, v, lam, subln_g, lam_init):
    S = k1.shape[1]
    slopes = jnp.asarray(ALIBI_SLOPES, jnp.float32)
    kpos = jnp.arange(S)
    scale = HEAD_DIM ** -0.5

    def block(args):
        q1b, q2b, blk = args
        qpos = blk * Q_BLOCK + jnp.arange(Q_BLOCK)
        dist = jnp.abs(qpos[:, None] - kpos[None, :]).astype(jnp.float32)
        bias = -slopes[:, None, None] * dist
        allowed = (kpos[None, :] // CHUNK) <= (qpos[:, None] // CHUNK)

        def probs(qb, k):
            s = jnp.einsum('bqhd,bkhd->bhqk', qb, k).astype(jnp.float32) * scale + bias
            s = jnp.where(allowed, s, NEG_INF)
            return jax.nn.softmax(s, axis=-1)

        a = probs(q1b, k1) - lam * probs(q2b, k2)
        return jnp.einsum('bhqk,bkhd->bqhd', a.astype(v.dtype), v)

    nb = S // Q_BLOCK
    o = lax.map(block, (query_blocks(q1), query_blocks(q2), jnp.arange(nb)))
    o = rms_norm(o, subln_g) * (1.0 - lam_init)
    return merge_blocks(o)


def forgetting_attention(q, k, v, log_f):
    B, S, H, _ = q.shape
    F = jnp.cumsum(log_f, axis=1)
    F_k = jnp.transpose(F, (0, 2, 1))
    F_q = jnp.transpose(query_blocks(F), (0, 1, 3, 2))
    kpos = jnp.arange(S)
    scale = HEAD_DIM ** -0.5

    def block(args):
        qb, fq, blk = args
        qpos = blk * Q_BLOCK + jnp.arange(Q_BLOCK)
        s = jnp.einsum('bqhd,bkhd->bhqk', qb, k).astype(jnp.float32) * scale
        s = s + fq[..., None] - F_k[:, :, None, :]
        allowed = kpos[None, :] <= qpos[:, None]
        s = jnp.where(allowed, s, NEG_INF)
        pr = jax.nn.softmax(s, axis=-1)
        return jnp.einsum('bhqk,bkhd->bqhd', pr.astype(v.dtype), v)

    nb = S // Q_BLOCK
    o = lax.map(block, (query_blocks(q), F_q, jnp.arange(nb)))
    return merge_blocks(o)


def chunk_band(t):
    b, s, h, d = t.shape
    nc = s // CHUNK
    tc = t.reshape(b, nc, CHUNK, h, d)
    tp = jnp.pad(tc, ((0, 0), (N_PREV_CHUNKS, 0), (0, 0), (0, 0), (0, 0)))
    return jnp.concatenate([tp[:, j:j + nc] for j in range(N_PREV_CHUNKS + 1)], axis=2)


def chunk_attention(q, k, v, rel_table):
    B, S, H, d = q.shape
    nc = S // CHUNK
    band_len = (N_PREV_CHUNKS + 1) * CHUNK
    qc = q.reshape(B, nc, CHUNK, H, d)
    kb = chunk_band(k)
    vb = chunk_band(v)
    qq = np.arange(CHUNK)[:, None]
    kk = np.arange(band_len)[None, :]
    rel = N_PREV_CHUNKS * CHUNK + qq - kk
    idx = np.clip(rel, -REL_CLIP, REL_CLIP) + REL_CLIP
    bias = rel_table[:, idx].astype(jnp.float32)
    key_chunk = (jnp.arange(nc)[:, None] - N_PREV_CHUNKS
                 + (jnp.arange(band_len) // CHUNK)[None, :])
    valid = (key_chunk >= 0)[:, None, None, :]
    s = jnp.einsum('bcqhd,bckhd->bchqk', qc, kb).astype(jnp.float32) * (HEAD_DIM ** -0.5) + bias
    s = jnp.where(valid, s, NEG_INF)
    pr = jax.nn.softmax(s, axis=-1)
    o = jnp.einsum('bchqk,bckhd->bcqhd', pr.astype(v.dtype), vb)
    return o.reshape(B, S, H * d)


def conv_gated_mlp(h, w_up, conv_w, conv_b, w_down):
    u = h @ w_up
    S = u.shape[1]
    up = jnp.pad(u, ((0, 0), (CONV_WIDTH - 1, 0), (0, 0)))
    c = conv_b
    for j in range(CONV_WIDTH):
        c = c + conv_w[j] * up[:, j:j + S]
    gate, val = jnp.split(c, 2, axis=-1)
    return (jax.nn.silu(gate) * val) @ w_down


def setup_inputs(seed: int = 0) -> dict:
    key = jax.random.key(seed)
    ks = jax.random.split(key, 20)
    f32 = jnp.float32
    nrm = lambda k, shape, s: jax.random.normal(k, shape, f32) * s
    gain = lambda k, shape: 1.0 + 0.05 * jax.random.normal(k, shape, f32)
    return {
        "x": jax.random.normal(ks[0], (BATCH, SEQ, D_MODEL), f32),
        "p": jax.random.normal(ks[1], (DEPTH, BATCH, SEQ, PLE_DIM), f32),
        "ln_mix": gain(ks[2], (DEPTH, D_MODEL)),
        "w_in": nrm(ks[3], (DEPTH, D_MODEL, IN_COLS), D_MODEL ** -0.5),
        "qk_gain": gain(ks[4], (DEPTH, 6, HEAD_DIM)),
        "lam_params": nrm(ks[5], (DEPTH, 4, HEAD_DIM), 0.1),
        "subln_gain": gain(ks[6], (DEPTH, HEAD_DIM)),
        "fgate_bias": jax.random.uniform(ks[7], (DEPTH, H_FOX), f32, 1.0, 4.0),
        "rel_bias": nrm(ks[8], (DEPTH, H_CHUNK, 2 * REL_CLIP + 1), 0.1),
        "w_out": nrm(ks[9], (DEPTH, MIX_WIDTH, D_MODEL), MIX_WIDTH ** -0.5),
        "ln_ffn": gain(ks[10], (DEPTH, D_MODEL)),
        "w_up": nrm(ks[11], (DEPTH, D_MODEL, 2 * D_FF), D_MODEL ** -0.5),
        "conv_w": nrm(ks[12], (DEPTH, CONV_WIDTH, 2 * D_FF), CONV_WIDTH ** -0.5),
        "conv_b": nrm(ks[13], (DEPTH, 2 * D_FF), 0.02),
        "w_down": nrm(ks[14], (DEPTH, D_FF, D_MODEL), D_FF ** -0.5),
        "ln_ple": gain(ks[15], (DEPTH, D_MODEL)),
        "w_ple_gate": nrm(ks[16], (DEPTH, D_MODEL, D_MODEL), D_MODEL ** -0.5),
        "w_ple_proj": nrm(ks[17], (DEPTH, PLE_DIM, D_MODEL), PLE_DIM ** -0.5),
    }


def reference(x, p, ln_mix, w_in, qk_gain, lam_params, subln_gain, fgate_bias, rel_bias,
              w_out, ln_ffn, w_up, conv_w, conv_b, w_down, ln_ple, w_ple_gate, w_ple_proj):
    h = x
    for i in range(DEPTH):
        hn = rms_norm(h, ln_mix[i])
        z = hn @ w_in[i]
        (q1, q2, k1, k2, va, qf, kf, vf, fg, qc, kc, vc) = jnp.split(z, IN_SPLITS, axis=-1)
        g = qk_gain[i]
        q1 = rms_norm(split_heads(q1, H_DIFF), g[0])
        q2 = rms_norm(split_heads(q2, H_DIFF), g[0])
        k1 = rms_norm(split_heads(k1, H_DIFF), g[1])
        k2 = rms_norm(split_heads(k2, H_DIFF), g[1])
        va = split_heads(va, H_DIFF)
        lp = lam_params[i].astype(jnp.float32)
        lam_init = 0.8 - 0.6 * math.exp(-0.3 * i)
        lam = jnp.exp(jnp.sum(lp[0] * lp[1])) - jnp.exp(jnp.sum(lp[2] * lp[3])) + lam_init
        o_a = diff_attention(q1, q2, k1, k2, va, lam, subln_gain[i], lam_init)
        qf = rms_norm(split_heads(qf, H_FOX), g[2])
        kf = rms_norm(split_heads(kf, H_FOX), g[3])
        vf = split_heads(vf, H_FOX)
        log_f = jax.nn.log_sigmoid(fg.astype(jnp.float32) + fgate_bias[i].astype(jnp.float32))
        o_b = forgetting_attention(qf, kf, vf, log_f)
        qc = rms_norm(split_heads(qc, H_CHUNK), g[4])
        kc = rms_norm(split_heads(kc, H_CHUNK), g[5])
        vc = split_heads(vc, H_CHUNK)
        o_c = chunk_attention(qc, kc, vc, rel_bias[i])
        h = h + jnp.concatenate([o_a, o_b, o_c], axis=-1) @ w_out[i]
        h = h + conv_gated_mlp(rms_norm(h, ln_ffn[i]), w_up[i], conv_w[i], conv_b[i], w_down[i])
        gate = jax.nn.sigmoid(rms_norm(h, ln_ple[i]) @ w_ple_gate[i])
        h = h + (p[i] @ w_ple_proj[i]) * gate
    return h
```

```python
import contextlib
import math
import os
import numpy as np
import concourse.bass as bass
import concourse.mybir as mybir
from concourse.bass_utils import run_bass_kernel_spmd

F32 = mybir.dt.float32
BF16 = mybir.dt.bfloat16
AF = mybir.ActivationFunctionType
ALU = mybir.AluOpType
AX = mybir.AxisListType

S = 4096
D = 1024
NT = S // 128
DEPTH = 2
DFF = 2816
PLE = 256
INC = 3590
EPS = 1e-6
NEG = -30000.0
SLOPES = [2.0 ** (-8.0 * (h + 1) / 4) for h in range(4)]
ENGS = ("pe", "act", "dve", "pool", "sp")
STQ = os.environ.get("K_STQ", "sp")
LDQ = os.environ.get("K_LDQ", "act")


class Buf:
    __slots__ = ("name", "w", "r", "sem", "cnt", "psum", "persist", "pidx", "ws")

    def __init__(self, name, persist=False):
        self.name = name
        self.psum = False
        self.persist = persist
        self.pidx = None
        self.ws = {}
        self.w = None
        self.r = []
        self.sem = None
        self.cnt = 0


class Tl:
    def __init__(self, t, name):
        self.t = t
        self.b = Buf(name)

    def __getitem__(self, k):
        return self.t[k]


class Prog:
    def __init__(self, nc, stack):
        self.nc = nc
        self.stack = stack
        self.esem = {e: stack.enter_context(nc.semaphore("es_" + e)) for e in ENGS}
        self.ecnt = {e: 0 for e in ENGS}
        self.pending = {e: False for e in ENGS}
        self.seen = {e: {} for e in ENGS}
        self.ops = {e: [] for e in ENGS}
        self.dma_sems = {}
        self.nsem = 0
        self.pool = [[stack.enter_context(nc.semaphore("dp%d" % i)), 0] for i in range(20)]
        self.free_idx = list(range(20))
        self.used_idx = []

    def _wait(self, eng, sem, val):
        if self.seen[eng].get(sem, 0) >= val:
            return
        self.seen[eng][sem] = val
        self.ops[eng].append(lambda e, sem=sem, val=val: e.wait_ge(sem, val))

    def op(self, eng, fn, reads=(), writes=(), signal=True, dma=False, nowaw=False):
        self.nops = getattr(self, "nops", 0) + 1
        if self.nops > int(os.environ.get("K_MAXOPS", 10 ** 9)):
            return
        deps = {}
        own = self.esem[eng]
        for b in reads:
            b = b.b if isinstance(b, Tl) else b
            if b.w is not None:
                s, v = b.w
                deps[s] = max(deps.get(s, 0), v)
            for s, v in b.ws.items():
                deps[s] = max(deps.get(s, 0), v)
            if b.psum:
                for s, v in b.r:
                    if s is not own:
                        deps[s] = max(deps.get(s, 0), v)
        for b in writes:
            b = b.b if isinstance(b, Tl) else b
            if not nowaw:
                if b.w is not None:
                    s, v = b.w
                    deps[s] = max(deps.get(s, 0), v)
                for s, v in b.ws.items():
                    deps[s] = max(deps.get(s, 0), v)
            for s, v in b.r:
                deps[s] = max(deps.get(s, 0), v)
        for s, v in deps.items():
            if eng == "pe" and s is own:
                continue
            self._wait(eng, s, v)
        if dma:
            wb = writes[0].b if isinstance(writes[0], Tl) else writes[0]
            if wb.sem is None:
                if wb.persist:
                    wb.sem = self.stack.enter_context(self.nc.semaphore("ds%d" % self.nsem))
                    self.nsem += 1
                else:
                    wb.pidx = self.free_idx.pop(0)
                    self.used_idx.append(wb.pidx)
                    wb.sem, wb.cnt = self.pool[wb.pidx]
            wb.cnt += 16
            if wb.pidx is not None:
                self.pool[wb.pidx][1] = wb.cnt
            ev = (wb.sem, wb.cnt)
            self.dma_sems[wb.sem] = wb.cnt
            self.ops[eng].append(lambda e, fn=fn, sem=wb.sem: fn(e).then_inc(sem, 16))
        elif signal:
            self.ecnt[eng] += 1
            ev = (own, self.ecnt[eng])
            self.pending[eng] = False
            self.ops[eng].append(lambda e, fn=fn, sem=own: fn(e).then_inc(sem, 1))
        else:
            ev = (own, self.ecnt[eng] + 1)
            self.pending[eng] = True
            self.ops[eng].append(lambda e, fn=fn: fn(e))
        for b in reads:
            b = b.b if isinstance(b, Tl) else b
            b.r.append(ev)
            if len(b.r) > 64:
                m = {}
                for s, v in b.r:
                    m[s] = max(m.get(s, 0), v)
                b.r = list(m.items())
        for b in writes:
            b = b.b if isinstance(b, Tl) else b
            if nowaw:
                b.ws[ev[0]] = max(b.ws.get(ev[0], 0), ev[1])
            else:
                b.w = ev
                b.ws = {}
                b.r = []

    def cast(self, dst_ap, dst_tl, src_ap, src_tl):
        self.ncast = getattr(self, "ncast", 0) + 1
        eng = ("dve", "act", "dve", "act", "pool")[self.ncast % 5]
        if eng == "act":
            self.op("act", lambda e: e.activation(out=dst_ap, in_=src_ap, func=AF.Copy), reads=[src_tl], writes=[dst_tl], nowaw=True)
        else:
            self.op(eng, lambda e: e.tensor_copy(out=dst_ap, in_=src_ap), reads=[src_tl], writes=[dst_tl], nowaw=True)

    def barrier(self):
        for e in ("pe", "act", "dve", "pool"):
            if self.pending[e]:
                raise RuntimeError("pending non-signalled op on " + e)
        for e in ("pe", "act", "dve", "pool"):
            if self.ecnt[e] > 0:
                self._wait("sp", self.esem[e], self.ecnt[e])
        for s, v in self.dma_sems.items():
            self._wait("sp", s, v)
        self.ecnt["sp"] += 1
        n = self.ecnt["sp"]
        sem = self.esem["sp"]
        self.ops["sp"].append(lambda e, sem=sem: e.nop().then_inc(sem, 1))
        for e in ("pe", "act", "dve", "pool"):
            self._wait(e, sem, n)

    def emit(self):
        self.free_idx.extend(self.used_idx)
        self.used_idx = []
        nc = self.nc
        ops = self.ops
        self.nphase = getattr(self, "nphase", 0) + 1
        with nc.named_scope("ph%02d" % self.nphase), nc.Block() as block:
            @block.tensor
            def _(e):
                for f in ops["pe"]:
                    f(e)

            @block.scalar
            def _(e):
                for f in ops["act"]:
                    f(e)

            @block.vector
            def _(e):
                for f in ops["dve"]:
                    f(e)

            @block.gpsimd
            def _(e):
                for f in ops["pool"]:
                    f(e)

            @block.sync
            def _(e):
                for f in ops["sp"]:
                    f(e)
        self.ops = {e: [] for e in ENGS}


_UID = [0]


def _uid():
    _UID[0] += 1
    return _UID[0]


def bcast_ap(ap, n):
    return bass.AP(ap.tensor, ap.offset, [list(x) for x in ap.ap] + [[0, n]])


def build_program(depth=DEPTH, debug=False, phases=None):
    nc = bass.Bass("TRN2", target_bir_lowering=False)

    def din(name, shape, dt=F32):
        return nc.dram_tensor(name, shape, dt, kind="ExternalInput").ap()

    x = din("x", [S, D])
    pin = din("p", [DEPTH, S, PLE])
    ln_mix = din("ln_mix", [DEPTH, D])
    w_in = din("w_in", [DEPTH, D, INC])
    qk_gain = din("qk_gain", [DEPTH, 6, 64])
    lam_params = din("lam_params", [DEPTH, 4, 64])
    subln_gain = din("subln_gain", [DEPTH, 64])
    fgate_bias = din("fgate_bias", [DEPTH, 6])
    rel_bias = din("rel_bias", [DEPTH, 6, 257])
    w_out = din("w_out", [DEPTH, D, D])
    ln_ffn = din("ln_ffn", [DEPTH, D])
    w_up = din("w_up", [DEPTH, D, 2 * DFF])
    conv_w = din("conv_w", [DEPTH, 3, 2 * DFF])
    conv_b = din("conv_b", [DEPTH, 2 * DFF])
    w_down = din("w_down", [DEPTH, DFF, D])
    ln_ple = din("ln_ple", [DEPTH, D])
    w_ple_gate = din("w_ple_gate", [DEPTH, D, D])
    w_ple_proj = din("w_ple_proj", [DEPTH, PLE, D])
    out = nc.dram_tensor("out", [S, D], F32, kind="ExternalOutput").ap()

    skind = dict(kind="ExternalOutput") if debug else {}

    def dscr(name, shape, dt):
        return nc.dram_tensor(name, shape, dt, **skind).ap()

    qkT = dscr("qkT", [40, 64, S], BF16)
    vS = dscr("vS", [S, 1024], BF16)
    FdT = dscr("FdT", [6, S], F32)
    att = dscr("att", [S, 1024], BF16)
    XN = dscr("XN", [S, D], F32)
    Y = dscr("Y", [S, D], F32)
    Z = dscr("Z", [S, D], F32)
    W = dscr("W", [S, D], F32)
    ext = dscr("ext", [6, 1536], F32)
    dram_bufs = {n: Buf(n, persist=True) for n in ["qkT", "vS", "FdT", "att", "XN", "Y", "Z", "W", "ext", "out", "x"]}

    gstack = contextlib.ExitStack()
    with gstack:
        P = Prog(nc, gstack)

        def GT(name, shape, dt):
            return Tl(gstack.enter_context(nc.sbuf_tensor(name, shape, dt)), name)

        ident_bf = GT("ident_bf", [128, 128], BF16)
        ident_f = GT("ident_f", [128, 128], F32)
        J_f = GT("J_f", [128, 128], F32)
        triU = GT("triU", [128, 128], F32)
        ones_f = GT("ones_f", [128, 128], F32)
        Fneg = GT("Fneg", [128, NT, 6], F32)

        def sel(tl, pattern, base, cm, cmp=ALU.not_equal, init=0.0, fill=1.0):
            P.op("pool", lambda e: e.memset(tl[:], init), writes=[tl])
            P.op("pool", lambda e: e.affine_select(out=tl[:], in_=tl[:], pattern=pattern, compare_op=cmp,
                                                  fill=fill, base=base, channel_multiplier=cm), reads=[tl], writes=[tl])

        sel(ident_f, [[-1, 128]], 0, 1)
        sel(J_f, [[1, 128]], -127, 1)
        sel(triU, [[1, 128]], 0, -1, cmp=ALU.is_ge, init=1.0, fill=0.0)
        P.op("pool", lambda e: e.memset(ones_f[:], 1.0), writes=[ones_f])
        P.op("pool", lambda e: e.tensor_copy(out=ident_bf[:], in_=ident_f[:]), reads=[ident_f], writes=[ident_bf])
        P.barrier()
        P.emit()

        def rstd_op(out_ap, out_tl, in_ap, in_tl, inv_n):
            P.op("act", lambda e: e.activation(out=out_ap, in_=in_ap, func=AF.Ln, scale=inv_n, bias=EPS), reads=[in_tl], writes=[out_tl])
            P.op("act", lambda e: e.activation(out=out_ap, in_=out_ap, func=AF.Exp, scale=-0.5), reads=[out_tl], writes=[out_tl])

        def phase1(L, Xl, Xb):
            with contextlib.ExitStack() as st:
                def T(name, shape, dt, n=1):
                    r = [Tl(st.enter_context(nc.sbuf_tensor("%s_%d_%d" % (name, _uid(), i), shape, dt)), name) for i in range(n)]
                    return r if n > 1 else r[0]

                def PT(name, shape, dt, n=1):
                    r = [Tl(st.enter_context(nc.psum_tensor("%s_%d_%d" % (name, _uid(), i), shape, dt)), name) for i in range(n)]
                    for x_ in r:
                        x_.b.psum = True
                    return r if n > 1 else r[0]

                wbf = T("wbf", [128, 8, INC], BF16)
                stg = T("stg", [128, INC], F32, 3)
                gb = T("gb", [128, D], F32)
                qg = T("qg", [64, 6], F32)
                fb = T("fb", [128, 6], F32)
                hin = T("hin", [128, D], F32, 3)
                junk = T("junk", [128, D], BF16)
                ss = T("ss", [128, 1], F32, 2)
                rr = T("rr", [128, 1], F32, 2)
                hn = T("hn", [128, D], BF16, 2)
                hnT = T("hnT", [128, 8, 128], BF16, 2)
                zs = T("zs", [128, 512], F32, 3)
                sq = T("sq", [128, 512], F32)
                ssh = T("ssh", [128, 8], F32, 12)
                rh = T("rh", [128, 8], F32, 12)
                zn = T("zn", [128, 512], BF16, 12)
                qkt = T("qkt", [64, 8, 128], BF16, 8)
                vsb = T("vsb", [128, 512], BF16, 6)
                t6 = T("t6", [128, 6], F32, 2)
                e6 = T("e6", [128, 6], F32, 2)
                l6 = T("l6", [128, 6], F32, 2)
                fq6 = T("fq6", [128, 6], F32, 2)
                Lrun = T("Lrun", [128, 6], F32, 2)
                tp_ps = PT("tp_ps", [128, 1024], BF16, 2)
                z_ps = PT("z_ps", [128, 512], F32, 3)
                qt_ps = PT("qt_ps", [128, 1024], BF16, 2)
                cum_ps = PT("cum_ps", [128, 512], F32)
                l8 = T("l8", [128, 8], F32, 2)
                L8 = T("L8", [128, 8], F32, 2)
                cpy = T("cpy", [128, 8], F32, 2)

                for k in range(8):
                    sg = stg[k % 3]
                    P.op("sp", lambda e, sg=sg, k=k: e.dma_start(out=sg[:], in_=w_in[L, k * 128:(k + 1) * 128, :]),
                         writes=[sg], dma=True)
                    P.cast(wbf[:, k, :], wbf, sg[:], sg)
                P.op("sp", lambda e: e.dma_start(out=gb[:], in_=bass.AP(ln_mix.tensor, L * D, [[0, 128], [1, D]])),
                     writes=[gb], dma=True)
                P.op("sp", lambda e: e.dma_start(out=qg[:], in_=bass.AP(qk_gain.tensor, L * 384, [[1, 64], [64, 6]]),
                                                 allow_slow_non_contiguous=True), writes=[qg], dma=True)
                P.op("dve", lambda e: e.tensor_scalar_mul(out=bass.AP(qg.t, 0, [[6, 64], [2, 3]]),
                                                          in0=bass.AP(qg.t, 0, [[6, 64], [2, 3]]), scalar1=0.125),
                     reads=[qg], writes=[qg])
                P.op("sp", lambda e: e.dma_start(out=fb[:], in_=bass.AP(fgate_bias.tensor, L * 6, [[0, 128], [1, 6]])),
                     writes=[fb], dma=True)
                P.op("dve", lambda e: e.memset(Lrun[0][:], 0.0), writes=[Lrun[0]])

                slabs = [
                    (0, 512, "qk", (0, 0)),
                    (512, 512, "qk", (8, 1)),
                    (1024, 256, "v", 0),
                    (1280, 384, "qk", (16, 2)),
                    (1664, 384, "qk", (22, 3)),
                    (2048, 390, "vf", 256),
                    (2438, 384, "qk", (28, 4)),
                    (2822, 384, "qk", (34, 5)),
                    (3206, 384, "v", 640),
                ]
                cnt = {"z": 0, "zn": 0, "qk": 0, "v": 0, "s": 0, "qp": 0}
                pend_n = []
                NT1 = int(os.environ.get('K_NT', NT))

                def f_load(t):
                    h_ = hin[t % 3]
                    P.op(LDQ, lambda e: e.dma_start(out=h_[:], in_=Xl[t * 128:(t + 1) * 128, :]), reads=[Xb], writes=[h_], dma=True)

                def f_norm(t):
                    s2 = t % 2
                    h_, ss_, rr_, hn_ = hin[t % 3], ss[s2], rr[s2], hn[s2]
                    P.op("act", lambda e: e.activation(out=junk[:], in_=h_[:], func=AF.Square, accum_out=ss_[:]), reads=[h_], writes=[junk, ss_])
                    rstd_op(rr_[:], rr_, ss_[:], ss_, 1.0 / D)
                    P.op("dve", lambda e: e.scalar_tensor_tensor(out=hn_[:], in0=h_[:], scalar=rr_[:, 0:1], in1=gb[:], op0=ALU.mult, op1=ALU.mult),
                         reads=[h_, rr_, gb], writes=[hn_])

                def f_pe(t):
                    s2 = t % 2
                    hn_, hnT_, tp_ = hn[s2], hnT[s2], tp_ps[s2]
                    for k in range(8):
                        P.op("pe", lambda e, k=k: e.transpose(out=tp_[:, k * 128:(k + 1) * 128], in_=hn_[:, k * 128:(k + 1) * 128], identity=ident_bf[:]),
                             reads=[hn_, ident_bf], writes=[tp_], signal=(k == 7))
                    P.op("act", lambda e: e.activation(out=hnT_[:].rearrange("p a b -> p (a b)"), in_=tp_[:], func=AF.Copy), reads=[tp_], writes=[hnT_])

                def do_slab(t, c0, ncol, kind, info):
                    s2 = t % 2
                    hnT_ = hnT[s2]
                    zp = z_ps[cnt["z"] % 3]
                    cnt["z"] += 1
                    for k in range(8):
                        P.op("pe", lambda e, k=k: e.matmul(zp[:, 0:ncol], hnT_[:, k, :], wbf[:, k, c0:c0 + ncol], start=(k == 0), stop=(k == 7)),
                             reads=[hnT_, wbf], writes=[zp], signal=(k == 7))
                    while len(pend_n) > 0 and not (kind == "qk" and len(pend_n) == 0):
                        pend_n.pop(0)()
                    if kind == "qk":
                        ht0, gi = info
                        nh = ncol // 64
                        z_ = zs[cnt["s"] % 3]
                        cnt["s"] += 1
                        i2 = cnt["zn"] % 12
                        cnt["zn"] += 1
                        ssh_, rh_, zn_ = ssh[i2], rh[i2], zn[i2]
                        sq_ = zs[cnt["s"] % 3]
                        P.op("act", lambda e: e.activation(out=sq_[:, 0:ncol], in_=zp[:, 0:ncol], func=AF.Square), reads=[zp], writes=[sq_])
                        P.op("dve", lambda e: e.tensor_reduce(out=ssh_[:, 0:nh], in_=sq_[:, 0:ncol].rearrange("p (a b) -> p a b", b=64), axis=AX.X, op=ALU.add),
                             reads=[sq_], writes=[ssh_])

                        def norm_stage():
                            rstd_op(rh_[:, 0:nh], rh_, ssh_[:, 0:nh], ssh_, 1.0 / 64)
                            P.op("dve", lambda e: e.tensor_tensor(out=zn_[:, 0:ncol].rearrange("p (a b) -> p a b", b=64), in0=zp[:, 0:ncol].rearrange("p (a b) -> p a b", b=64),
                                                                  in1=bcast_ap(rh_[:, 0:nh], 64), op=ALU.mult), reads=[zp, rh_], writes=[zn_])
                        pend_n.append(norm_stage)

                        def late():
                            qp_ = qt_ps[cnt["qp"] % 2]
                            cnt["qp"] += 1
                            qk_ = qkt[cnt["qk"] % 8]
                            cnt["qk"] += 1
                            for hh in range(nh):
                                P.op("pe", lambda e, hh=hh: e.transpose(out=qp_[0:64, hh * 128:(hh + 1) * 128], in_=zn_[:, hh * 64:(hh + 1) * 64], identity=ident_bf[:]),
                                     reads=[zn_, ident_bf], writes=[qp_], signal=(hh == nh - 1))
                            P.op("dve", lambda e: e.tensor_scalar_mul(out=qk_[:, 0:nh, :].rearrange("p a b -> p (a b)"), in0=qp_[0:64, 0:nh * 128],
                                                                      scalar1=qg[:, gi:gi + 1]), reads=[qp_, qg], writes=[qk_])
                            if not os.environ.get("K_NOQKST"):
                                P.op(STQ, lambda e: e.dma_start(out=qkT[ht0:ht0 + nh, :, t * 128:(t + 1) * 128].rearrange("h d t -> d h t"), in_=qk_[:, 0:nh, :]),
                                     reads=[qk_], writes=[dram_bufs["qkT"]], dma=True)
                        return late
                    vc0 = info
                    nv = 256 if kind == "v" and ncol == 256 else 384
                    v_ = vsb[cnt["v"] % 6]
                    cnt["v"] += 1
                    P.op("act", lambda e: e.activation(out=v_[:, 0:nv], in_=zp[:, 0:nv], func=AF.Copy), reads=[zp], writes=[v_])
                    for (a0, a1) in ((0, 256), (256, nv)):
                        if a1 > a0:
                            P.op(STQ, lambda e, a0=a0, a1=a1: e.dma_start(out=vS[t * 128:(t + 1) * 128, vc0 + a0:vc0 + a1], in_=v_[:, a0:a1]),
                                 reads=[v_], writes=[dram_bufs["vS"]], dma=True)
                    if kind != "vf":
                        return None
                    t6_, e6_, l6_, fq_ = t6[s2], e6[s2], l6[s2], fq6[s2]
                    Lp, Ln_ = Lrun[t % 2], Lrun[(t + 1) % 2]
                    l8_, L8_, cp_ = l8[s2], L8[s2], cpy[s2]
                    P.op("dve", lambda e: e.tensor_tensor(out=t6_[:], in0=zp[:, 384:390], in1=fb[:], op=ALU.add), reads=[zp, fb], writes=[t6_])
                    P.op("act", lambda e: e.activation(out=e6_[:], in_=t6_[:], func=AF.Exp, scale=-1.0), reads=[t6_], writes=[e6_])
                    P.op("act", lambda e: e.activation(out=l6_[:], in_=e6_[:], func=AF.Ln, bias=1.0), reads=[e6_], writes=[l6_])
                    P.op("dve", lambda e: e.memset(l8_[:], 0.0), writes=[l8_])
                    P.op("dve", lambda e: e.tensor_copy(out=l8_[:, 0:6], in_=l6_[:]), reads=[l6_, l8_], writes=[l8_])
                    P.op("dve", lambda e: e.memset(L8_[:], 0.0), writes=[L8_])
                    P.op("dve", lambda e: e.tensor_copy(out=L8_[:, 0:6], in_=Lp[:]), reads=[Lp, L8_], writes=[L8_])
                    P.op("dve", lambda e: e.tensor_tensor(out=Ln_[:], in0=Lp[:], in1=l6_[:], op=ALU.add), reads=[Lp, l6_], writes=[Ln_])

                    def late():
                        P.op("pe", lambda e: e.matmul(cum_ps[:, 0:8], triU[:], l8_[:], start=True, stop=True), reads=[l8_, triU], writes=[cum_ps])
                        P.op("pe", lambda e: e.matmul(cum_ps[:, 8:16], ones_f[:], L8_[:], start=True, stop=True), reads=[L8_, ones_f], writes=[cum_ps])
                        P.op("dve", lambda e: e.tensor_copy(out=cp_[:], in_=cum_ps[:, 8:16]), reads=[cum_ps], writes=[cp_])
                        P.op("dve", lambda e: e.tensor_tensor(out=Fneg[:, t, :], in0=cum_ps[:, 0:6], in1=cp_[:, 0:6], op=ALU.add), reads=[cum_ps, cp_], writes=[Fneg])
                        P.op("dve", lambda e: e.tensor_scalar_mul(out=fq_[:], in0=Fneg[:, t, :], scalar1=-1.0), reads=[Fneg], writes=[fq_])
                        P.op(STQ, lambda e: e.dma_start(out=bass.AP(FdT.tensor, t * 128, [[1, 128], [S, 6]]), in_=fq_[:], allow_slow_non_contiguous=True),
                             reads=[fq_], writes=[dram_bufs["FdT"]], dma=True)
                    return late

                f_load(0)
                if NT1 > 1:
                    f_load(1)
                f_norm(0)
                f_pe(0)
                prev = []
                for t in range(NT1):
                    if t + 2 < NT1:
                        f_load(t + 2)
                    if t + 1 < NT1:
                        f_norm(t + 1)
                    cur = []
                    for si, (c0, ncol, kind, info) in enumerate(slabs):
                        late = do_slab(t, c0, ncol, kind, info)
                        if late is not None:
                            cur.append(late)
                        if prev:
                            prev.pop(0)()
                        if si == 5 and t + 1 < NT1:
                            f_pe(t + 1)
                    while prev:
                        prev.pop(0)()
                    prev = cur
                while prev:
                    prev.pop(0)()
                P.barrier()
                P.emit()

        def attention(L):
            def run_group(kind):
                with contextlib.ExitStack() as st:
                    def T(name, shape, dt, n=1):
                        r = [Tl(st.enter_context(nc.sbuf_tensor("%s_%d_%d" % (name, _uid(), i), shape, dt)), name) for i in range(n)]
                        return r if n > 1 else r[0]

                    def PT(name, shape, dt, n=1):
                        r = [Tl(st.enter_context(nc.psum_tensor("%s_%d_%d" % (name, _uid(), i), shape, dt)), name) for i in range(n)]
                        for x_ in r:
                            x_.b.psum = True
                        return r if n > 1 else r[0]

                    qkr = T("qkr", [128, S], BF16, 4)
                    Vt = T("Vt", [128, NT, 65], BF16, 4)
                    tmp = T("tmp", [128, 512], F32, 8)
                    PTt = T("PTt", [128, 512], BF16, 6)
                    osb = T("osb", [65, 512], F32, 2)
                    onrm = T("onrm", [128, 4, 64], F32, 6)
                    rec = T("rec", [128, 4], F32, 2)
                    obf = T("obf", [128, 4, 64], BF16, 4)
                    w1 = T("w1", [128, 512], F32)
                    w2 = T("w2", [128, 512], F32)
                    w3 = T("w3", [128, 512], F32)
                    w4 = T("w4", [128, 512], F32)
                    ps_s = PT("ps_s", [128, 512], F32, 4)
                    acc = PT("acc", [128, 512], F32, 2)
                    otp = PT("otp", [128, 512], F32, 1)
                    otp = [otp, otp]
                    ctr = {"qk": 0, "v": 0, "ps": 0, "tmp": 0, "pt": 0, "acc": 0, "osb": 0, "on": 0, "rec": 0, "obf": 0}

                    for v_ in Vt:
                        P.op("dve", lambda e, v_=v_: e.memset(v_[:, :, 64:65], 1.0), writes=[v_])
                    P.op("dve", lambda e: e.memset(w2[:], NEG), writes=[w2])

                    def sel_neg(dst, p0, p1, pattern, base, cm_):
                        if len(pattern) == 2:
                            o_ = dst[p0:p1, :].rearrange("p (a b) -> p a b", b=pattern[1][1])
                            i_ = w2[p0:p1, :].rearrange("p (a b) -> p a b", b=pattern[1][1])
                        else:
                            o_, i_ = dst[p0:p1, :], w2[p0:p1, :]
                        P.op("pool", lambda e: e.affine_select(out=o_, in_=i_, pattern=pattern, compare_op=ALU.is_ge, fill=0.0, base=base,
                                                              channel_multiplier=cm_), reads=[w2], writes=[dst])

                    def flip(dst_ap, dst_tl, src):
                        P.op("dve", lambda e: e.tensor_scalar(out=dst_ap, in0=src[:], scalar1=-1.0, scalar2=NEG, op0=ALU.mult, op1=ALU.add),
                             reads=[src], writes=[dst_tl])

                    def load_pair(htA, htB):
                        tl = qkr[ctr["qk"] % 4]
                        ctr["qk"] += 1
                        P.op("sp", lambda e: e.dma_start(out=tl[0:64, :], in_=qkT[htA, :, :]), reads=[dram_bufs["qkT"]], writes=[tl], dma=True)
                        P.op("sp", lambda e: e.dma_start(out=tl[64:128, :], in_=qkT[htB, :, :]), reads=[dram_bufs["qkT"]], writes=[tl], dma=True)
                        return tl

                    def load_v(vcol):
                        tl = Vt[ctr["v"] % 4]
                        ctr["v"] += 1
                        P.op("sp", lambda e, tl=tl, vcol=vcol: e.dma_start(out=tl[:, :, 0:64], in_=vS[:, vcol:vcol + 64].rearrange("(n p) c -> p n c", p=128)),
                             reads=[dram_bufs["vS"]], writes=[tl], dma=True)
                        return tl

                    S1, S2 = 3, 1
                    jobs = []

                    def attn_pair(Qp, Kp, qt, subs):
                        accs = []
                        for _ in subs:
                            accs.append(acc[ctr["acc"] % 2])
                            ctr["acc"] += 1
                        n = len(subs[0][1])
                        for idx in range(n):
                            sj = []
                            for half, (V, ktiles, cb) in enumerate(subs):
                                kt, b1, b2, ab = ktiles[idx]
                                sj.append(dict(V=V, kt=kt, b1=b1, b2=b2, ab=ab, a_=accs[half], cb=cb, first=(idx == 0), last=(idx == n - 1)))
                            jobs.append(dict(Qp=Qp, Kp=Kp, qt=qt, kt=subs[0][1][idx][0], subs=sj))

                    def st_qk_dve(jb):
                        Qp, Kp, qt, kt = jb["Qp"], jb["Kp"], jb["qt"], jb["kt"]
                        pss = []
                        for half in range(2):
                            ps = ps_s[ctr["ps"] % 4]
                            ctr["ps"] += 1
                            pss.append(ps)
                            lo = 64 * half
                            P.op("pe", lambda e, ps=ps, lo=lo: e.matmul(ps[:], Kp[lo:lo + 64, kt * 128:(kt + 1) * 128], Qp[lo:lo + 64, qt * 512:(qt + 1) * 512],
                                                                       start=True, stop=True), reads=[Kp, Qp], writes=[ps])
                        for half in range(2):
                            sb = jb["subs"][half]
                            ps = pss[half]
                            tm = tmp[ctr["tmp"] % 8]
                            ctr["tmp"] += 1
                            sb["tm"] = tm
                            b1, b2 = sb["b1"], sb["b2"]
                            P.op("dve", lambda e, ps=ps, tm=tm, b1=b1: e.tensor_tensor(out=tm[:], in0=ps[:], in1=b1[0], op=ALU.add), reads=[ps, b1[1]], writes=[tm])
                            if b2 is not None:
                                P.op("dve", lambda e, tm=tm, b2=b2: e.tensor_tensor(out=tm[:], in0=tm[:], in1=b2[0], op=ALU.add), reads=[tm, b2[1]], writes=[tm])

                    def st_act(jb):
                        for sb in jb["subs"]:
                            tm, ab = sb["tm"], sb["ab"]
                            pt = PTt[ctr["pt"] % 6]
                            ctr["pt"] += 1
                            sb["pt"] = pt
                            if isinstance(ab, tuple):
                                P.op("act", lambda e, tm=tm, pt=pt, ab=ab: e.activation(out=pt[:], in_=tm[:], func=AF.Exp, bias=ab[0]), reads=[tm, ab[1]], writes=[pt])
                            else:
                                P.op("act", lambda e, tm=tm, pt=pt, ab=ab: e.activation(out=pt[:], in_=tm[:], func=AF.Exp, bias=float(ab)), reads=[tm], writes=[pt])

                    def st_pv(jb):
                        for sb in jb["subs"]:
                            a_, V, kt, pt, first, last = sb["a_"], sb["V"], sb["kt"], sb["pt"], sb["first"], sb["last"]
                            P.op("pe", lambda e, a_=a_, V=V, kt=kt, pt=pt, first=first, last=last: e.matmul(a_[0:65, :], V[:, kt, :], pt[:], start=first, stop=last),
                                 reads=[V, pt], writes=[a_])
                        for sb in jb["subs"]:
                            if sb["last"]:
                                finalize(sb)

                    def finalize(sb):
                        a_ = sb["a_"]
                        o_ = osb[ctr["osb"] % 2]
                        ctr["osb"] += 1
                        op_ = otp[0]
                        on_ = onrm[ctr["on"] % 6]
                        ctr["on"] += 1
                        rc_ = rec[ctr["rec"] % 2]
                        ctr["rec"] += 1
                        P.op("act", lambda e: e.activation(out=o_[:], in_=a_[0:65, :], func=AF.Copy), reads=[a_], writes=[o_])
                        for s_ in range(4):
                            P.op("pe", lambda e, s_=s_: e.transpose(out=op_[:, s_ * 65:(s_ + 1) * 65], in_=o_[:, s_ * 128:(s_ + 1) * 128], identity=ident_f[0:65, 0:65]),
                                 reads=[o_, ident_f], writes=[op_], signal=(s_ == 3))
                        P.op("dve", lambda e: e.reciprocal(out=rc_[:], in_=op_[:, 0:260].rearrange("p (a b) -> p a b", b=65)[:, :, 64]), reads=[op_], writes=[rc_])
                        P.op("dve", lambda e: e.tensor_tensor(out=on_[:], in0=op_[:, 0:260].rearrange("p (a b) -> p a b", b=65)[:, :, 0:64], in1=bcast_ap(rc_[:, 0:4], 64), op=ALU.mult),
                             reads=[op_, rc_], writes=[on_])
                        sb["cb"](on_)

                    def flush():
                        n = len(jobs)
                        for i in range(min(S1, n)):
                            st_qk_dve(jobs[i])
                        for i in range(min(S2, n)):
                            st_act(jobs[i])
                        for i in range(n):
                            if i + S1 < n:
                                st_qk_dve(jobs[i + S1])
                            if i + S2 < n:
                                st_act(jobs[i + S2])
                            st_pv(jobs[i])
                        del jobs[:]

                    def store_o(ob_, qt, col):
                        P.op(STQ, lambda e: e.dma_start(out=att[qt * 512:(qt + 1) * 512, col:col + 64].rearrange("(s p) c -> p s c", p=128), in_=ob_[:]),
                             reads=[ob_], writes=[dram_bufs["att"]], dma=True)

                    if kind == "diff":
                        T0 = T("T0", [128, 4, 512], F32)
                        Bd = T("Bd", [128, 16, 512], F32)
                        lamp = T("lamp", [128, 4, 64], F32)
                        lpr = T("lpr", [128, 2, 64], F32)
                        ls = T("ls", [128, 2], F32)
                        le = T("le", [128, 2], F32)
                        nlam = T("nlam", [128, 1], F32)
                        gsub = T("gsub", [128, 64], F32)
                        od = T("od", [128, 4, 64], F32, 2)
                        osq = T("osq", [128, 4, 64], F32)
                        oss = T("oss", [128, 4], F32, 2)
                        ors = T("ors", [128, 4], F32, 2)
                        lam_init = 0.8 - 0.6 * math.exp(-0.3 * L)
                        P.op("sp", lambda e: e.dma_start(out=lamp[:].rearrange("p a b -> p (a b)"), in_=bass.AP(lam_params.tensor, L * 256, [[0, 128], [1, 256]])),
                             writes=[lamp], dma=True)
                        P.op("dve", lambda e: e.tensor_tensor(out=lpr[:], in0=bass.AP(lamp.t, 0, [[256, 128], [128, 2], [1, 64]]),
                                                              in1=bass.AP(lamp.t, 64, [[256, 128], [128, 2], [1, 64]]), op=ALU.mult), reads=[lamp], writes=[lpr])
                        P.op("dve", lambda e: e.tensor_reduce(out=ls[:], in_=lpr[:], axis=AX.X, op=ALU.add), reads=[lpr], writes=[ls])
                        P.op("act", lambda e: e.activation(out=le[:], in_=ls[:], func=AF.Exp), reads=[ls], writes=[le])
                        P.op("dve", lambda e: e.scalar_tensor_tensor(out=nlam[:], in0=le[:, 1:2], scalar=-lam_init, in1=le[:, 0:1], op0=ALU.add, op1=ALU.subtract),
                             reads=[le], writes=[nlam])
                        P.op("sp", lambda e: e.dma_start(out=gsub[:], in_=bass.AP(subln_gain.tensor, L * 64, [[0, 128], [1, 64]])), writes=[gsub], dma=True)
                        P.op("dve", lambda e: e.tensor_scalar_mul(out=gsub[:], in0=gsub[:], scalar1=(1.0 - lam_init)), reads=[gsub], writes=[gsub])
                        P.op("pool", lambda e: e.iota(w1[:], [[1, 512]], base=0, channel_multiplier=-1, allow_small_or_imprecise_dtypes=True), writes=[w1])
                        for h in range(4):
                            P.op("dve", lambda e, h=h: e.tensor_scalar_mul(out=T0[:, h, :], in0=w1[:], scalar1=-SLOPES[h]), reads=[w1], writes=[T0])
                        for j in range(4):
                            for half in range(2):
                                sel_neg(w3, 64 * half, 64 * half + 64, [[64, 8], [0, 64]], -(128 * j + 64 * half), 0)
                            flip(w3[:], w3, w3)
                            P.op("pool", lambda e, j=j: e.iota(w1[:], [[1, 512]], base=-128 * j, channel_multiplier=-1, allow_small_or_imprecise_dtypes=True), writes=[w1])
                            P.op("dve", lambda e: e.tensor_scalar_mul(out=w4[:], in0=w1[:], scalar1=-1.0), reads=[w1], writes=[w4])
                            P.op("dve", lambda e: e.tensor_tensor(out=w1[:], in0=w1[:], in1=w4[:], op=ALU.max), reads=[w1, w4], writes=[w1])
                            for h in range(4):
                                P.op("dve", lambda e, h=h, j=j: e.scalar_tensor_tensor(out=Bd[:, h * 4 + j, :], in0=w1[:], scalar=-SLOPES[h], in1=w3[:],
                                                                                      op0=ALU.mult, op1=ALU.add), reads=[w1, w3], writes=[Bd])
                        def load_head(h):
                            return (load_pair(h, 4 + h), load_pair(8 + h, 12 + h), load_v(h * 64))

                        def combine(o1, o2, qt, h):
                            i2 = ctr["obf"] % 2
                            ctr["obf"] += 1
                            od_, oss_, ors_, ob_ = od[i2], oss[i2], ors[i2], obf[i2]
                            P.op("dve", lambda e: e.scalar_tensor_tensor(out=od_[:], in0=o2[:], scalar=nlam[:, 0:1], in1=o1[:], op0=ALU.mult, op1=ALU.add),
                                 reads=[o1, o2, nlam], writes=[od_])
                            P.op("dve", lambda e: e.tensor_tensor(out=osq[:], in0=od_[:], in1=od_[:], op=ALU.mult), reads=[od_], writes=[osq])
                            P.op("dve", lambda e: e.tensor_reduce(out=oss_[:], in_=osq[:], axis=AX.X, op=ALU.add), reads=[osq], writes=[oss_])
                            rstd_op(ors_[:], ors_, oss_[:], oss_, 1.0 / 64)
                            P.op("dve", lambda e: e.tensor_tensor(out=od_[:], in0=od_[:], in1=bcast_ap(ors_[:, 0:4], 64), op=ALU.mult), reads=[od_, ors_], writes=[od_])
                            P.op("dve", lambda e: e.tensor_tensor(out=ob_[:], in0=od_[:], in1=bass.AP(gsub.t, 0, [[64, 128], [0, 4], [1, 64]]), op=ALU.mult),
                                 reads=[od_, gsub], writes=[ob_])
                            store_o(ob_, qt, h * 64)

                        nxt = load_head(0)
                        for h in range(4):
                            Qp, Kp, V = nxt
                            if h + 1 < 4:
                                nxt = load_head(h + 1)
                            for qt in range(8):
                                box = []

                                def fin(on_, box=box, qt=qt, h=h):
                                    box.append(on_)
                                    if len(box) == 2:
                                        combine(box[0], box[1], qt, h)
                                kts = []
                                for kt in range(4 * qt + 4):
                                    if kt < 4 * qt:
                                        kts.append((kt, (T0[:, h, :], T0), None, -SLOPES[h] * (512 * qt - 128 * kt)))
                                    else:
                                        kts.append((kt, (Bd[:, h * 4 + (kt - 4 * qt), :], Bd), None, 0.0))
                                attn_pair(Qp, Kp, qt, [(V, kts, fin), (V, kts, fin)])
                            flush()

                    elif kind == "fox":
                        FQb = T("FQb", [128, S], F32, 4)
                        caus = T("caus", [128, 4, 512], F32)
                        for j in range(4):
                            sel_neg(w3, 0, 128, [[1, 512]], -128 * j, -1)
                            flip(caus[:, j, :], caus, w3)
                        def cast_store(col):
                            def fin(on_, qt):
                                ob_ = obf[ctr["obf"] % 2]
                                ctr["obf"] += 1
                                P.op("dve", lambda e: e.tensor_copy(out=ob_[:], in_=on_[:]), reads=[on_], writes=[ob_])
                                store_o(ob_, qt, col)
                            return fin

                        fqc = [0]

                        def load_head(h0):
                            Qp, Kp = load_pair(16 + h0, 17 + h0), load_pair(22 + h0, 23 + h0)
                            out_ = [Qp, Kp]
                            for h in (h0, h0 + 1):
                                V = load_v(256 + h * 64)
                                fq = FQb[fqc[0] % 4]
                                fqc[0] += 1
                                P.op("sp", lambda e, fq=fq, h=h: e.dma_start(out=fq[:], in_=bass.AP(FdT.tensor, h * S, [[0, 128], [1, S]])),
                                     reads=[dram_bufs["FdT"]], writes=[fq], dma=True)
                                out_.append((V, fq))
                            return out_

                        nxt = load_head(0)
                        for h0 in (0, 2, 4):
                            Qp, Kp, hA, hB = nxt
                            if h0 + 2 < 6:
                                nxt = load_head(h0 + 2)
                            for qt in range(8):
                                subs = []
                                for h, (V, fq) in ((h0, hA), (h0 + 1, hB)):
                                    fin_h = cast_store(256 + h * 64)
                                    kts = []
                                    for kt in range(4 * qt + 4):
                                        b2 = None if kt < 4 * qt else (caus[:, kt - 4 * qt, :], caus)
                                        kts.append((kt, (fq[:, qt * 512:(qt + 1) * 512], fq), b2, (Fneg[:, kt, h:h + 1], Fneg)))
                                    subs.append((V, kts, lambda on_, qt=qt, fin_h=fin_h: fin_h(on_, qt)))
                                attn_pair(Qp, Kp, qt, subs)
                            flush()

                    else:
                        cm = T("cm", [128, 8, 512], F32)
                        Bc = T("Bc", [128, 8, 512], F32, 4)
                        Tp = T("Tp", [128, 512], F32, 2)
                        jps = otp[0]
                        eb = dram_bufs["ext"]
                        exs = T("exs", [6, 1536], F32)
                        P.op("sp", lambda e: e.dma_start(out=exs[:, 383:640], in_=rel_bias[L, :, :]), writes=[exs], dma=True)
                        P.op("dve", lambda e: e.tensor_copy(out=exs[:, 0:383], in_=bass.AP(exs.t, 383, [[1536, 6], [0, 383]])), reads=[exs], writes=[exs])
                        P.op("dve", lambda e: e.tensor_copy(out=exs[:, 640:1536], in_=bass.AP(exs.t, 639, [[1536, 6], [0, 896]])), reads=[exs], writes=[exs])
                        P.op("sp", lambda e: e.dma_start(out=ext[:, :], in_=exs[:]), reads=[exs], writes=[eb], dma=True)
                        for j in range(8):
                            for half in range(2):
                                kc = 128 * j + 64 * half
                                sel_neg(w3, 64 * half, 64 * half + 64, [[64, 8], [0, 64]], 512 - kc, 0)
                                sel_neg(w1, 64 * half, 64 * half + 64, [[64, 8], [0, 64]], -kc - 64, 0)
                            flip(w3[:], w3, w3)
                            P.op("dve", lambda e, j=j: e.tensor_tensor(out=cm[:, j, :], in0=w3[:], in1=w1[:], op=ALU.add), reads=[w3, w1], writes=[cm])
                        tpc = [0]

                        def cast_store(col):
                            def fin(on_, qt):
                                ob_ = obf[ctr["obf"] % 2]
                                ctr["obf"] += 1
                                P.op("dve", lambda e: e.tensor_copy(out=ob_[:], in_=on_[:]), reads=[on_], writes=[ob_])
                                store_o(ob_, qt, col)
                            return fin

                        bcc = [0]

                        def load_head(h0):
                            Qp, Kp = load_pair(28 + h0, 29 + h0), load_pair(34 + h0, 35 + h0)
                            out_ = [Qp, Kp]
                            for h in (h0, h0 + 1):
                                V = load_v(640 + h * 64)
                                bc = Bc[bcc[0] % 4]
                                bcc[0] += 1
                                for j in range(8):
                                    tp_ = Tp[tpc[0] % 2]
                                    tpc[0] += 1
                                    P.op("sp", lambda e, tp_=tp_, j=j, h=h: e.dma_start(out=tp_[:], in_=bass.AP(ext.tensor, h * 1536 + 896 - 128 * j, [[1, 128], [1, 512]])),
                                         reads=[eb], writes=[tp_], dma=True)
                                    P.op("pe", lambda e, tp_=tp_: e.matmul(jps[:], J_f[:], tp_[:], start=True, stop=True), reads=[tp_, J_f], writes=[jps])
                                    P.op("dve", lambda e, j=j, bc=bc: e.tensor_tensor(out=bc[:, j, :], in0=jps[:], in1=cm[:, j, :], op=ALU.add), reads=[jps, cm], writes=[bc])
                                out_.append((V, bc))
                            return out_

                        nxt = load_head(0)
                        for h0 in (0, 2, 4):
                            Qp, Kp, hA, hB = nxt
                            if h0 + 2 < 6:
                                nxt = load_head(h0 + 2)
                            for qt in range(8):
                                subs = []
                                for h, (V, bc) in ((h0, hA), (h0 + 1, hB)):
                                    fin_h = cast_store(640 + h * 64)
                                    kts = []
                                    for j in range(8):
                                        kt = 4 * qt - 4 + j
                                        if kt < 0:
                                            continue
                                        kts.append((kt, (bc[:, j, :], bc), None, 0.0))
                                    subs.append((V, kts, lambda on_, qt=qt, fin_h=fin_h: fin_h(on_, qt)))
                                attn_pair(Qp, Kp, qt, subs)
                            flush()
                    P.barrier()
                    P.emit()

            for kind in ("diff", "fox", "chunk"):
                run_group(kind)

        def load_weight_bf(st, name, src_rows, nrow_chunks, ncols, col0=0, row0=0, stg=None):
            wt = Tl(st.enter_context(nc.sbuf_tensor("%s_%d" % (name, _uid()), [128, nrow_chunks, ncols], BF16)), name)
            nsub = (ncols + 2815) // 2816
            if stg is None:
                stg = [Tl(st.enter_context(nc.sbuf_tensor("%s_stg%d_%d" % (name, i, _uid()), [128, min(ncols, 2816)], F32)), name + "stg") for i in range(3)]
            c = 0
            for k in range(nrow_chunks):
                for sb in range(nsub):
                    a = sb * 2816
                    w = min(2816, ncols - a)
                    sg = stg[c % len(stg)]
                    c += 1
                    P.op("sp", lambda e, sg=sg, k=k, a=a, w=w: e.dma_start(out=sg[:, 0:w], in_=src_rows[row0 + k * 128:row0 + (k + 1) * 128, col0 + a:col0 + a + w]),
                         writes=[sg], dma=True)
                    P.cast(wt[:, k, a:a + w], wt, sg[:, 0:w], sg)
            return wt

        def norm_T(T_, PT_, tiles, h_, gbt, idx):
            s2 = idx % 2
            ss_, rr_, hn_, tp_ = tiles["ss"][s2], tiles["rr"][s2], tiles["hn"][s2], tiles["tp"][s2]
            junk = tiles["junk"]
            P.op("act", lambda e: e.activation(out=junk[:], in_=h_[:], func=AF.Square, accum_out=ss_[:]), reads=[h_], writes=[junk, ss_])
            rstd_op(rr_[:], rr_, ss_[:], ss_, 1.0 / D)
            P.op("dve", lambda e: e.scalar_tensor_tensor(out=hn_[:], in0=h_[:], scalar=rr_[:, 0:1], in1=gbt[:], op0=ALU.mult, op1=ALU.mult),
                 reads=[h_, rr_, gbt], writes=[hn_])
            for k in range(8):
                P.op("pe", lambda e, k=k: e.transpose(out=tp_[:, k * 128:(k + 1) * 128], in_=hn_[:, k * 128:(k + 1) * 128], identity=ident_bf[:]),
                     reads=[hn_, ident_bf], writes=[tp_], signal=(k == 7))
            return tp_

        def norm_tiles(T, PT):
            return dict(ss=T("ss", [128, 1], F32, 2), rr=T("rr", [128, 1], F32, 2), hn=T("hn", [128, D], BF16, 2),
                        hnT=None, tp=PT("tp", [128, 1024], BF16, 2), junk=T("junk", [128, D], BF16))

        def load_gain(T, name, src, L):
            g = T(name, [128, D], F32)
            P.op("sp", lambda e: e.dma_start(out=g[:], in_=bass.AP(src.tensor, L * D, [[0, 128], [1, D]])), writes=[g], dma=True)
            return g

        def mk_alloc(st):
            def T(name, shape, dt, n=1):
                r = [Tl(st.enter_context(nc.sbuf_tensor("%s_%d_%d" % (name, _uid(), i), shape, dt)), name) for i in range(n)]
                return r if n > 1 else r[0]

            def PT(name, shape, dt, n=1):
                r = [Tl(st.enter_context(nc.psum_tensor("%s_%d_%d" % (name, _uid(), i), shape, dt)), name) for i in range(n)]
                for x_ in r:
                    x_.b.psum = True
                return r if n > 1 else r[0]
            return T, PT

        def phase3(L, Xl, Xb, Yd, Yb):
            with contextlib.ExitStack() as st:
                T, PT = mk_alloc(st)
                wo = load_weight_bf(st, "wo", w_out[L], 8, D)
                a_in = T("a_in", [128, D], BF16, 2)
                h_in = T("h_in", [128, D], F32, 2)
                aT = T("aT", [128, 8, 128], BF16, 2)
                h_o = T("h_o", [128, D], F32, 2)
                tp = PT("tp", [128, 1024], BF16, 2)
                ops_ = PT("ops", [128, 512], F32, 4)

                def f_load(t):
                    s2 = t % 2
                    P.op(LDQ, lambda e: e.dma_start(out=a_in[s2][:], in_=att[t * 128:(t + 1) * 128, :]), reads=[dram_bufs["att"]], writes=[a_in[s2]], dma=True)
                    P.op(LDQ, lambda e: e.dma_start(out=h_in[s2][:], in_=Xl[t * 128:(t + 1) * 128, :]), reads=[Xb], writes=[h_in[s2]], dma=True)

                def front(t):
                    s2 = t % 2
                    for k in range(8):
                        P.op("pe", lambda e, k=k: e.transpose(out=tp[s2][:, k * 128:(k + 1) * 128], in_=a_in[s2][:, k * 128:(k + 1) * 128], identity=ident_bf[:]),
                             reads=[a_in[s2], ident_bf], writes=[tp[s2]], signal=(k == 7))
                    P.op("act", lambda e: e.activation(out=aT[s2][:].rearrange("p a b -> p (a b)"), in_=tp[s2][:], func=AF.Copy), reads=[tp[s2]], writes=[aT[s2]])

                def back(t, mid=None):
                    s2 = t % 2
                    if t + 1 < NT:
                        f_load(t + 1)
                    for n in range(2):
                        if n == 1 and mid is not None:
                            mid()
                        ps = ops_[(2 * t + n) % 4]
                        for k in range(8):
                            P.op("pe", lambda e, k=k, n=n, ps=ps: e.matmul(ps[:], aT[s2][:, k, :], wo[:, k, n * 512:(n + 1) * 512], start=(k == 0), stop=(k == 7)),
                                 reads=[aT[s2], wo], writes=[ps], signal=(k == 7))
                        P.op("dve", lambda e, n=n, ps=ps: e.tensor_tensor(out=h_o[s2][:, n * 512:(n + 1) * 512], in0=ps[:], in1=h_in[s2][:, n * 512:(n + 1) * 512], op=ALU.add),
                             reads=[ps, h_in[s2]], writes=[h_o[s2]])
                    P.op(STQ, lambda e: e.dma_start(out=Yd[t * 128:(t + 1) * 128, :], in_=h_o[s2][:]), reads=[h_o[s2]], writes=[Yb], dma=True)

                f_load(0)
                front(0)
                for t in range(NT):
                    back(t, (lambda t=t: front(t + 1)) if t + 1 < NT else None)
                P.barrier()
                P.emit()

        def phase4(L, hf, Yd, Yb, Ad, Ab, Od, Ob):
            with contextlib.ExitStack() as st:
                T, PT = mk_alloc(st)
                NCH = 11
                gcol0 = hf * NCH * 128
                vcol0 = DFF + hf * NCH * 128
                stg4 = T("stg4", [128, NCH * 128], F32, 3)
                wug = load_weight_bf(st, "wug", w_up[L], 8, NCH * 128, col0=gcol0, stg=stg4)
                wuv = load_weight_bf(st, "wuv", w_up[L], 8, NCH * 128, col0=vcol0, stg=stg4)
                wd = load_weight_bf(st, "wd", w_down[L], NCH, D, row0=hf * NCH * 128, stg=stg4)
                gbt = load_gain(T, "gbt", ln_ffn, L)
                ss4 = T("ss4", [128, 1], F32, 4)
                rr4 = T("rr4", [128, 1], F32, 4)
                hn4 = T("hn4", [128, D], BF16, 4)
                junk4 = T("junk4", [128, D], BF16)
                tp4 = PT("tp4", [128, 1024], BF16, 2)
                cw = T("cw", [128, 3, 2 * NCH], F32)
                cb = T("cb", [128, 2 * NCH], F32)
                for j in range(3):
                    for (g0, base) in ((0, gcol0), (NCH, vcol0)):
                        P.op("sp", lambda e, j=j, g0=g0, base=base: e.dma_start(
                            out=cw[:, j, g0:g0 + NCH], in_=bass.AP(conv_w.tensor, (L * 3 + j) * 2 * DFF + base, [[1, 128], [128, NCH]]),
                            allow_slow_non_contiguous=True), writes=[cw], dma=True)
                for (g0, base) in ((0, gcol0), (NCH, vcol0)):
                    P.op("sp", lambda e, g0=g0, base=base: e.dma_start(out=cb[:, g0:g0 + NCH], in_=bass.AP(conv_b.tensor, L * 2 * DFF + base, [[1, 128], [128, NCH]]),
                                                                       allow_slow_non_contiguous=True), writes=[cb], dma=True)
                h_in = T("h_in", [128, D], F32, 4)
                a_in = T("a_in", [128, 4, D], F32)
                hnT = T("hnT", [128, 8, 512], BF16, 2)
                ub = T("ub", [128, 514], F32, 3)
                halo = T("halo", [128, 2 * NCH, 2], F32)
                c0 = T("c0", [128, 512], F32, 2)
                c1 = T("c1", [128, 512], F32, 2)
                sg_ = T("sg", [128, 512], F32, 2)
                gT = T("gT", [128, NCH, 512], BF16, 2)
                h_o = T("h_o", [128, D], F32, 2)
                u_ps = PT("u_ps", [128, 512], F32, 3)
                d_ps = PT("d_ps", [128, 512], F32, 3)
                P.op("dve", lambda e: e.memset(halo[:], 0.0), writes=[halo])
                ctr = {"u": 0, "d": 0}
                NT4 = S // 512

                def f_load(T4):
                    for s in range(4):
                        t = T4 * 4 + s
                        hi = h_in[s]
                        P.op(LDQ, lambda e, t=t, hi=hi: e.dma_start(out=hi[:], in_=Yd[t * 128:(t + 1) * 128, :]), reads=[Yb], writes=[hi], dma=True)

                def front(T4):
                    for s in range(4):
                        hi = h_in[s]
                        P.op("act", lambda e, s=s, hi=hi: e.activation(out=junk4[:], in_=hi[:], func=AF.Square, accum_out=ss4[s][:]), reads=[hi], writes=[junk4, ss4[s]])
                    for s in range(4):
                        rstd_op(rr4[s][:], rr4[s], ss4[s][:], ss4[s], 1.0 / D)
                    for s in range(4):
                        hi = h_in[s]
                        P.op("dve", lambda e, s=s, hi=hi: e.scalar_tensor_tensor(out=hn4[s][:], in0=hi[:], scalar=rr4[s][:, 0:1], in1=gbt[:], op0=ALU.mult, op1=ALU.mult),
                             reads=[hi, rr4[s], gbt], writes=[hn4[s]])

                def front_b(T4):
                    hT = hnT[T4 % 2]
                    for s in range(4):
                        tp_ = tp4[s % 2]
                        for k in range(8):
                            P.op("pe", lambda e, k=k, s=s, tp_=tp_: e.transpose(out=tp_[:, k * 128:(k + 1) * 128], in_=hn4[s][:, k * 128:(k + 1) * 128], identity=ident_bf[:]),
                                 reads=[hn4[s], ident_bf], writes=[tp_], signal=(k == 7))
                        P.op("act", lambda e, s=s, tp_=tp_: e.activation(out=hT[:, :, s * 128:(s + 1) * 128], in_=tp_[:].rearrange("p (a b) -> p a b", a=8), func=AF.Copy),
                             reads=[tp_], writes=[hT])

                def back(T4, mid=None, mid_b=None):
                    b2 = T4 % 2
                    hT = hnT[b2]
                    g_ = gT[b2]
                    P.op(LDQ, lambda e: e.dma_start(out=a_in[:], in_=Ad[T4 * 512:(T4 + 1) * 512, :].rearrange("(s p) c -> p s c", p=128)),
                         reads=[Ab], writes=[a_in], dma=True)
                    if T4 + 1 < NT4:
                        f_load(T4 + 1)
                    for c in range(NCH):
                        if c == 1 and mid is not None:
                            mid()
                        if c == 6 and mid_b is not None:
                            mid_b()
                        res = []
                        for (wt, ci) in ((wug, c), (wuv, NCH + c)):
                            ps = u_ps[ctr["u"] % 3]
                            u_ = ub[ctr["u"] % 3]
                            ctr["u"] += 1
                            for k in range(8):
                                P.op("pe", lambda e, k=k, ps=ps, wt=wt, c=c: e.matmul(ps[:], wt[:, k, c * 128:(c + 1) * 128], hT[:, k, :], start=(k == 0), stop=(k == 7)),
                                     reads=[hT, wt], writes=[ps], signal=(k == 7))
                            a0, a1 = c0[ci // NCH], c1[ci // NCH]
                            P.op("act", lambda e, u_=u_, ci=ci: e.activation(out=u_[:, 0:2], in_=halo[:, ci, :], func=AF.Copy), reads=[halo], writes=[u_])
                            P.op("act", lambda e, ps=ps, u_=u_: e.activation(out=u_[:, 2:514], in_=ps[:], func=AF.Copy), reads=[ps], writes=[u_])
                            P.op("act", lambda e, ps=ps, ci=ci: e.activation(out=halo[:, ci, :], in_=ps[:, 510:512], func=AF.Copy), reads=[ps], writes=[halo])
                            P.op("act", lambda e, ps=ps, a0=a0, ci=ci: e.activation(out=a0[:], in_=ps[:], func=AF.Copy, scale=cw[:, 2, ci:ci + 1]),
                                 reads=[ps, cw], writes=[a0])
                            P.op("dve", lambda e, u_=u_, a0=a0, a1=a1, ci=ci: e.scalar_tensor_tensor(out=a1[:], in0=u_[:, 1:513], scalar=cw[:, 1, ci:ci + 1], in1=a0[:],
                                                                                                op0=ALU.mult, op1=ALU.add), reads=[u_, cw, a0], writes=[a1])
                            P.op("dve", lambda e, u_=u_, a0=a0, a1=a1, ci=ci: e.scalar_tensor_tensor(out=a0[:], in0=u_[:, 0:512], scalar=cw[:, 0, ci:ci + 1], in1=a1[:],
                                                                                                op0=ALU.mult, op1=ALU.add), reads=[u_, cw, a1], writes=[a0])
                            res.append(a0)
                        gt_, vl_ = res
                        sgt = sg_[c % 2]
                        P.op("act", lambda e, gt_=gt_, sgt=sgt, c=c: e.activation(out=sgt[:], in_=gt_[:], func=AF.Silu, bias=cb[:, c:c + 1]), reads=[gt_, cb], writes=[sgt])
                        P.op("dve", lambda e, sgt=sgt, vl_=vl_, c=c: e.scalar_tensor_tensor(out=g_[:, c, :], in0=vl_[:], scalar=cb[:, NCH + c:NCH + c + 1], in1=sgt[:],
                                                                                           op0=ALU.add, op1=ALU.mult), reads=[sgt, vl_, cb], writes=[g_])
                    for s in range(4):
                        t = T4 * 4 + s
                        ho = h_o[t % 2]
                        for n in range(2):
                            ps = d_ps[ctr["d"] % 3]
                            ctr["d"] += 1
                            for c in range(NCH):
                                P.op("pe", lambda e, c=c, n=n, s=s, ps=ps: e.matmul(ps[:], g_[:, c, s * 128:(s + 1) * 128], wd[:, c, n * 512:(n + 1) * 512],
                                                                                  start=(c == 0), stop=(c == NCH - 1)),
                                     reads=[g_, wd], writes=[ps], signal=(c == NCH - 1))
                            P.op("dve", lambda e, n=n, s=s, ps=ps, ho=ho: e.tensor_tensor(out=ho[:, n * 512:(n + 1) * 512], in0=ps[:], in1=a_in[:, s, n * 512:(n + 1) * 512], op=ALU.add),
                                 reads=[ps, a_in], writes=[ho])
                        P.op(STQ, lambda e, t=t, ho=ho: e.dma_start(out=Od[t * 128:(t + 1) * 128, :], in_=ho[:]), reads=[ho], writes=[Ob], dma=True)

                f_load(0)
                front(0)
                front_b(0)
                for T4 in range(NT4):
                    back(T4, (lambda T4=T4: front(T4 + 1)) if T4 + 1 < NT4 else None, (lambda T4=T4: front_b(T4 + 1)) if T4 + 1 < NT4 else None)
                P.barrier()
                P.emit()

        def phase5(L, Wd, Wb, Od, Ob):
            with contextlib.ExitStack() as st:
                T, PT = mk_alloc(st)
                stg5 = T("stg5", [128, D], F32, 3)
                wg = load_weight_bf(st, "wg", w_ple_gate[L], 8, D, stg=stg5)
                wp = load_weight_bf(st, "wp", w_ple_proj[L], 2, D, stg=stg5)
                gbt = load_gain(T, "gbt", ln_ple, L)
                nt = norm_tiles(T, PT)
                h_in = T("h_in", [128, D], F32, 2)
                p_in = T("p_in", [128, PLE], F32, 2)
                p_bf = T("p_bf", [128, PLE], BF16, 2)
                hnT = T("hnT", [128, 8, 128], BF16, 2)
                pT = T("pT", [128, 2, 128], BF16, 2)
                gate = T("gate", [128, D], F32, 2)
                pg = T("pg", [128, D], F32, 2)
                h_o = T("h_o", [128, D], F32, 2)
                ptp = PT("ptp", [128, 1024], BF16, 1)
                g_ps = PT("g_ps", [128, 512], F32, 2)
                p_ps = PT("p_ps", [128, 512], F32, 2)

                def f_load(t):
                    s2 = t % 2
                    hi, pi = h_in[s2], p_in[s2]
                    P.op(LDQ, lambda e: e.dma_start(out=hi[:], in_=Wd[t * 128:(t + 1) * 128, :]), reads=[Wb], writes=[hi], dma=True)
                    P.op(LDQ, lambda e: e.dma_start(out=pi[:], in_=pin[L, t * 128:(t + 1) * 128, :]), writes=[pi], dma=True)

                def front(t):
                    s2 = t % 2
                    hi, pi, pb, hT, pT_ = h_in[s2], p_in[s2], p_bf[s2], hnT[s2], pT[s2]
                    tp_ = norm_T(T, PT, nt, hi, gbt, t)
                    P.op("act", lambda e: e.activation(out=hT[:].rearrange("p a b -> p (a b)"), in_=tp_[:], func=AF.Copy), reads=[tp_], writes=[hT])
                    P.op("pool", lambda e: e.tensor_copy(out=pb[:], in_=pi[:]), reads=[pi], writes=[pb])
                    for k in range(2):
                        P.op("pe", lambda e, k=k: e.transpose(out=ptp[:, k * 128:(k + 1) * 128], in_=pb[:, k * 128:(k + 1) * 128], identity=ident_bf[:]),
                             reads=[pb, ident_bf], writes=[ptp], signal=(k == 1))
                    P.op("act", lambda e: e.activation(out=pT_[:].rearrange("p a b -> p (a b)"), in_=ptp[:, 0:256], func=AF.Copy), reads=[ptp], writes=[pT_])

                def back(t, mid=None):
                    s2 = t % 2
                    hi, hT, pT_, gt, pg_, ho = h_in[s2], hnT[s2], pT[s2], gate[s2], pg[s2], h_o[s2]
                    if t + 1 < NT:
                        f_load(t + 1)
                    for n in range(2):
                        if n == 1 and mid is not None:
                            mid()
                        gp, pp = g_ps[n], p_ps[n]
                        for k in range(8):
                            P.op("pe", lambda e, k=k, n=n, gp=gp: e.matmul(gp[:], hT[:, k, :], wg[:, k, n * 512:(n + 1) * 512], start=(k == 0), stop=(k == 7)),
                                 reads=[hT, wg], writes=[gp], signal=(k == 7))
                        P.op("act", lambda e, n=n, gp=gp: e.activation(out=gt[:, n * 512:(n + 1) * 512], in_=gp[:], func=AF.Sigmoid), reads=[gp], writes=[gt])
                        for k in range(2):
                            P.op("pe", lambda e, k=k, n=n, pp=pp: e.matmul(pp[:], pT_[:, k, :], wp[:, k, n * 512:(n + 1) * 512], start=(k == 0), stop=(k == 1)),
                                 reads=[pT_, wp], writes=[pp], signal=(k == 1))
                        P.op("dve", lambda e, n=n, pp=pp: e.tensor_tensor(out=pg_[:, n * 512:(n + 1) * 512], in0=pp[:], in1=gt[:, n * 512:(n + 1) * 512], op=ALU.mult),
                             reads=[pp, gt], writes=[pg_])
                        P.op("pool", lambda e, n=n: e.tensor_tensor(out=ho[:, n * 512:(n + 1) * 512], in0=pg_[:, n * 512:(n + 1) * 512],
                                                                   in1=hi[:, n * 512:(n + 1) * 512], op=ALU.add), reads=[pg_, hi], writes=[ho])
                    P.op(STQ, lambda e: e.dma_start(out=Od[t * 128:(t + 1) * 128, :], in_=ho[:]), reads=[ho], writes=[Ob], dma=True)

                f_load(0)
                front(0)
                for t in range(NT):
                    back(t, (lambda t=t: front(t + 1)) if t + 1 < NT else None)
                P.barrier()
                P.emit()

        db = dram_bufs
        for L in range(depth):
            Xl, Xb = (x, db["x"]) if L == 0 else (XN, db["XN"])
            last = (L == depth - 1)
            Od, Ob = (out, db["out"]) if last else (XN, db["XN"])
            if phases is None or 1 in phases:
                phase1(L, Xl, Xb)
            if phases is None or 2 in phases:
                attention(L)
            if phases is None or 3 in phases:
                phase3(L, Xl, Xb, Y, db["Y"])
            if phases is None or 4 in phases:
                phase4(L, 0, Y, db["Y"], Y, db["Y"], Z, db["Z"])
                phase4(L, 1, Y, db["Y"], Z, db["Z"], W, db["W"])
            if phases is None or 5 in phases:
                phase5(L, W, db["W"], Od, Ob)
    return nc


_CACHE = {}


def kernel(**inputs):
    names = ["ln_mix", "w_in", "qk_gain", "lam_params", "subln_gain", "fgate_bias", "rel_bias", "w_out", "ln_ffn", "w_up",
             "conv_w", "conv_b", "w_down", "ln_ple", "w_ple_gate", "w_ple_proj"]
    x = np.ascontiguousarray(np.asarray(inputs["x"], dtype=np.float32))
    p = np.asarray(inputs["p"], dtype=np.float32)
    shared = {n: np.ascontiguousarray(np.asarray(inputs[n], dtype=np.float32)) for n in names}
    if "nc" not in _CACHE:
        _CACHE["nc"] = build_program()
    nc = _CACHE["nc"]
    in_maps = []
    for b in range(8):
        m = {"x": x[b], "p": np.ascontiguousarray(p[:, b])}
        m.update(shared)
        in_maps.append(m)
    res = run_bass_kernel_spmd(nc, in_maps, core_ids=list(range(8)))
    return np.stack([np.asarray(r["out"], dtype=np.float32) for r in res.results], axis=0)
```

```python
import contextlib
import math
import os
import numpy as np
import concourse.bass as bass
import concourse.mybir as mybir
from concourse.bass_utils import run_bass_kernel_spmd

F32 = mybir.dt.float32
BF16 = mybir.dt.bfloat16
AF = mybir.ActivationFunctionType
ALU = mybir.AluOpType
AX = mybir.AxisListType

S = 4096
D = 1024
NT = S // 128
DEPTH = 2
DFF = 2816
PLE = 256
INC = 3590
EPS = 1e-6
NEG = -30000.0
SLOPES = [2.0 ** (-8.0 * (h + 1) / 4) for h in range(4)]
ENGS = ("pe", "act", "dve", "pool", "sp")
STQ = os.environ.get("K_STQ", "sp")


class Buf:
    __slots__ = ("name", "w", "r", "sem", "cnt", "psum", "persist", "pidx", "ws")

    def __init__(self, name, persist=False):
        self.name = name
        self.psum = False
        self.persist = persist
        self.pidx = None
        self.ws = {}
        self.w = None
        self.r = []
        self.sem = None
        self.cnt = 0


class Tl:
    def __init__(self, t, name):
        self.t = t
        self.b = Buf(name)

    def __getitem__(self, k):
        return self.t[k]


class Prog:
    def __init__(self, nc, stack):
        self.nc = nc
        self.stack = stack
        self.esem = {e: stack.enter_context(nc.semaphore("es_" + e)) for e in ENGS}
        self.ecnt = {e: 0 for e in ENGS}
        self.pending = {e: False for e in ENGS}
        self.seen = {e: {} for e in ENGS}
        self.ops = {e: [] for e in ENGS}
        self.dma_sems = {}
        self.nsem = 0
        self.pool = [[stack.enter_context(nc.semaphore("dp%d" % i)), 0] for i in range(20)]
        self.free_idx = list(range(20))
        self.used_idx = []

    def _wait(self, eng, sem, val):
        if self.seen[eng].get(sem, 0) >= val:
            return
        self.seen[eng][sem] = val
        self.ops[eng].append(lambda e, sem=sem, val=val: e.wait_ge(sem, val))

    def op(self, eng, fn, reads=(), writes=(), signal=True, dma=False, nowaw=False):
        self.nops = getattr(self, "nops", 0) + 1
        if self.nops > int(os.environ.get("K_MAXOPS", 10 ** 9)):
            return
        deps = {}
        own = self.esem[eng]
        for b in reads:
            b = b.b if isinstance(b, Tl) else b
            if b.w is not None:
                s, v = b.w
                deps[s] = max(deps.get(s, 0), v)
            for s, v in b.ws.items():
                deps[s] = max(deps.get(s, 0), v)
            if b.psum:
                for s, v in b.r:
                    if s is not own:
                        deps[s] = max(deps.get(s, 0), v)
        for b in writes:
            b = b.b if isinstance(b, Tl) else b
            if not nowaw:
                if b.w is not None:
                    s, v = b.w
                    deps[s] = max(deps.get(s, 0), v)
                for s, v in b.ws.items():
                    deps[s] = max(deps.get(s, 0), v)
            for s, v in b.r:
                deps[s] = max(deps.get(s, 0), v)
        for s, v in deps.items():
            if eng == "pe" and s is own:
                continue
            self._wait(eng, s, v)
        if dma:
            wb = writes[0].b if isinstance(writes[0], Tl) else writes[0]
            if wb.sem is None:
                if wb.persist:
                    wb.sem = self.stack.enter_context(self.nc.semaphore("ds%d" % self.nsem))
                    self.nsem += 1
                else:
                    wb.pidx = self.free_idx.pop(0)
                    self.used_idx.append(wb.pidx)
                    wb.sem, wb.cnt = self.pool[wb.pidx]
            wb.cnt += 16
            if wb.pidx is not None:
                self.pool[wb.pidx][1] = wb.cnt
            ev = (wb.sem, wb.cnt)
            self.dma_sems[wb.sem] = wb.cnt
            self.ops[eng].append(lambda e, fn=fn, sem=wb.sem: fn(e).then_inc(sem, 16))
        elif signal:
            self.ecnt[eng] += 1
            ev = (own, self.ecnt[eng])
            self.pending[eng] = False
            self.ops[eng].append(lambda e, fn=fn, sem=own: fn(e).then_inc(sem, 1))
        else:
            ev = (own, self.ecnt[eng] + 1)
            self.pending[eng] = True
            self.ops[eng].append(lambda e, fn=fn: fn(e))
        for b in reads:
            b = b.b if isinstance(b, Tl) else b
            b.r.append(ev)
            if len(b.r) > 64:
                m = {}
                for s, v in b.r:
                    m[s] = max(m.get(s, 0), v)
                b.r = list(m.items())
        for b in writes:
            b = b.b if isinstance(b, Tl) else b
            if nowaw:
                b.ws[ev[0]] = max(b.ws.get(ev[0], 0), ev[1])
            else:
                b.w = ev
                b.ws = {}
                b.r = []

    def cast(self, dst_ap, dst_tl, src_ap, src_tl):
        self.ncast = getattr(self, "ncast", 0) + 1
        eng = ("dve", "act", "dve", "act", "pool")[self.ncast % 5]
        if eng == "act":
            self.op("act", lambda e: e.activation(out=dst_ap, in_=src_ap, func=AF.Copy), reads=[src_tl], writes=[dst_tl], nowaw=True)
        else:
            self.op(eng, lambda e: e.tensor_copy(out=dst_ap, in_=src_ap), reads=[src_tl], writes=[dst_tl], nowaw=True)

    def barrier(self):
        for e in ("pe", "act", "dve", "pool"):
            if self.pending[e]:
                raise RuntimeError("pending non-signalled op on " + e)
        for e in ("pe", "act", "dve", "pool"):
            if self.ecnt[e] > 0:
                self._wait("sp", self.esem[e], self.ecnt[e])
        for s, v in self.dma_sems.items():
            self._wait("sp", s, v)
        self.ecnt["sp"] += 1
        n = self.ecnt["sp"]
        sem = self.esem["sp"]
        self.ops["sp"].append(lambda e, sem=sem: e.nop().then_inc(sem, 1))
        for e in ("pe", "act", "dve", "pool"):
            self._wait(e, sem, n)

    def emit(self):
        self.free_idx.extend(self.used_idx)
        self.used_idx = []
        nc = self.nc
        ops = self.ops
        self.nphase = getattr(self, "nphase", 0) + 1
        with nc.named_scope("ph%02d" % self.nphase), nc.Block() as block:
            @block.tensor
            def _(e):
                for f in ops["pe"]:
                    f(e)

            @block.scalar
            def _(e):
                for f in ops["act"]:
                    f(e)

            @block.vector
            def _(e):
                for f in ops["dve"]:
                    f(e)

            @block.gpsimd
            def _(e):
                for f in ops["pool"]:
                    f(e)

            @block.sync
            def _(e):
                for f in ops["sp"]:
                    f(e)
        self.ops = {e: [] for e in ENGS}


_UID = [0]


def _uid():
    _UID[0] += 1
    return _UID[0]


def bcast_ap(ap, n):
    return bass.AP(ap.tensor, ap.offset, [list(x) for x in ap.ap] + [[0, n]])


def build_program(depth=DEPTH, debug=False, phases=None):
    nc = bass.Bass("TRN2", target_bir_lowering=False)

    def din(name, shape, dt=F32):
        return nc.dram_tensor(name, shape, dt, kind="ExternalInput").ap()

    x = din("x", [S, D])
    pin = din("p", [DEPTH, S, PLE])
    ln_mix = din("ln_mix", [DEPTH, D])
    w_in = din("w_in", [DEPTH, D, INC])
    qk_gain = din("qk_gain", [DEPTH, 6, 64])
    lam_params = din("lam_params", [DEPTH, 4, 64])
    subln_gain = din("subln_gain", [DEPTH, 64])
    fgate_bias = din("fgate_bias", [DEPTH, 6])
    rel_bias = din("rel_bias", [DEPTH, 6, 257])
    w_out = din("w_out", [DEPTH, D, D])
    ln_ffn = din("ln_ffn", [DEPTH, D])
    w_up = din("w_up", [DEPTH, D, 2 * DFF])
    conv_w = din("conv_w", [DEPTH, 3, 2 * DFF])
    conv_b = din("conv_b", [DEPTH, 2 * DFF])
    w_down = din("w_down", [DEPTH, DFF, D])
    ln_ple = din("ln_ple", [DEPTH, D])
    w_ple_gate = din("w_ple_gate", [DEPTH, D, D])
    w_ple_proj = din("w_ple_proj", [DEPTH, PLE, D])
    out = nc.dram_tensor("out", [S, D], F32, kind="ExternalOutput").ap()

    skind = dict(kind="ExternalOutput") if debug else {}

    def dscr(name, shape, dt):
        return nc.dram_tensor(name, shape, dt, **skind).ap()

    qkT = dscr("qkT", [40, 64, S], BF16)
    vS = dscr("vS", [S, 1024], BF16)
    FdT = dscr("FdT", [6, S], F32)
    att = dscr("att", [S, 1024], BF16)
    XN = dscr("XN", [S, D], F32)
    Y = dscr("Y", [S, D], F32)
    Z = dscr("Z", [S, D], F32)
    W = dscr("W", [S, D], F32)
    ext = dscr("ext", [6, 1536], F32)
    dram_bufs = {n: Buf(n, persist=True) for n in ["qkT", "vS", "FdT", "att", "XN", "Y", "Z", "W", "ext", "out", "x"]}

    gstack = contextlib.ExitStack()
    with gstack:
        P = Prog(nc, gstack)

        def GT(name, shape, dt):
            return Tl(gstack.enter_context(nc.sbuf_tensor(name, shape, dt)), name)

        ident_bf = GT("ident_bf", [128, 128], BF16)
        ident_f = GT("ident_f", [128, 128], F32)
        J_f = GT("J_f", [128, 128], F32)
        triU = GT("triU", [128, 128], F32)
        ones_f = GT("ones_f", [128, 128], F32)
        Fneg = GT("Fneg", [128, NT, 6], F32)

        def sel(tl, pattern, base, cm, cmp=ALU.not_equal, init=0.0, fill=1.0):
            P.op("pool", lambda e: e.memset(tl[:], init), writes=[tl])
            P.op("pool", lambda e: e.affine_select(out=tl[:], in_=tl[:], pattern=pattern, compare_op=cmp,
                                                  fill=fill, base=base, channel_multiplier=cm), reads=[tl], writes=[tl])

        sel(ident_f, [[-1, 128]], 0, 1)
        sel(J_f, [[1, 128]], -127, 1)
        sel(triU, [[1, 128]], 0, -1, cmp=ALU.is_ge, init=1.0, fill=0.0)
        P.op("pool", lambda e: e.memset(ones_f[:], 1.0), writes=[ones_f])
        P.op("pool", lambda e: e.tensor_copy(out=ident_bf[:], in_=ident_f[:]), reads=[ident_f], writes=[ident_bf])
        P.barrier()
        P.emit()

        def rstd_op(out_ap, out_tl, in_ap, in_tl, inv_n):
            P.op("act", lambda e: e.activation(out=out_ap, in_=in_ap, func=AF.Ln, scale=inv_n, bias=EPS), reads=[in_tl], writes=[out_tl])
            P.op("act", lambda e: e.activation(out=out_ap, in_=out_ap, func=AF.Exp, scale=-0.5), reads=[out_tl], writes=[out_tl])

        def phase1(L, Xl, Xb):
            with contextlib.ExitStack() as st:
                def T(name, shape, dt, n=1):
                    r = [Tl(st.enter_context(nc.sbuf_tensor("%s_%d_%d" % (name, _uid(), i), shape, dt)), name) for i in range(n)]
                    return r if n > 1 else r[0]

                def PT(name, shape, dt, n=1):
                    r = [Tl(st.enter_context(nc.psum_tensor("%s_%d_%d" % (name, _uid(), i), shape, dt)), name) for i in range(n)]
                    for x_ in r:
                        x_.b.psum = True
                    return r if n > 1 else r[0]

                wbf = T("wbf", [128, 8, INC], BF16)
                stg = T("stg", [128, INC], F32, 3)
                gb = T("gb", [128, D], F32)
                qg = T("qg", [64, 6], F32)
                fb = T("fb", [128, 6], F32)
                hin = T("hin", [128, D], F32, 3)
                junk = T("junk", [128, D], BF16)
                ss = T("ss", [128, 1], F32, 2)
                rr = T("rr", [128, 1], F32, 2)
                hn = T("hn", [128, D], BF16, 2)
                hnT = T("hnT", [128, 8, 128], BF16, 2)
                zs = T("zs", [128, 512], F32, 3)
                sq = T("sq", [128, 512], F32)
                ssh = T("ssh", [128, 8], F32, 12)
                rh = T("rh", [128, 8], F32, 12)
                zn = T("zn", [128, 512], BF16, 12)
                qkt = T("qkt", [64, 8, 128], BF16, 8)
                vsb = T("vsb", [128, 512], BF16, 6)
                t6 = T("t6", [128, 6], F32, 2)
                e6 = T("e6", [128, 6], F32, 2)
                l6 = T("l6", [128, 6], F32, 2)
                fq6 = T("fq6", [128, 6], F32, 2)
                Lrun = T("Lrun", [128, 6], F32, 2)
                tp_ps = PT("tp_ps", [128, 1024], BF16, 2)
                z_ps = PT("z_ps", [128, 512], F32, 3)
                qt_ps = PT("qt_ps", [128, 1024], BF16, 2)
                cum_ps = PT("cum_ps", [128, 512], F32)
                l8 = T("l8", [128, 8], F32, 2)
                L8 = T("L8", [128, 8], F32, 2)
                cpy = T("cpy", [128, 8], F32, 2)

                for k in range(8):
                    sg = stg[k % 3]
                    P.op("sp", lambda e, sg=sg, k=k: e.dma_start(out=sg[:], in_=w_in[L, k * 128:(k + 1) * 128, :]),
                         writes=[sg], dma=True)
                    P.cast(wbf[:, k, :], wbf, sg[:], sg)
                P.op("sp", lambda e: e.dma_start(out=gb[:], in_=bass.AP(ln_mix.tensor, L * D, [[0, 128], [1, D]])),
                     writes=[gb], dma=True)
                P.op("sp", lambda e: e.dma_start(out=qg[:], in_=bass.AP(qk_gain.tensor, L * 384, [[1, 64], [64, 6]]),
                                                 allow_slow_non_contiguous=True), writes=[qg], dma=True)
                P.op("dve", lambda e: e.tensor_scalar_mul(out=bass.AP(qg.t, 0, [[6, 64], [2, 3]]),
                                                          in0=bass.AP(qg.t, 0, [[6, 64], [2, 3]]), scalar1=0.125),
                     reads=[qg], writes=[qg])
                P.op("sp", lambda e: e.dma_start(out=fb[:], in_=bass.AP(fgate_bias.tensor, L * 6, [[0, 128], [1, 6]])),
                     writes=[fb], dma=True)
                P.op("dve", lambda e: e.memset(Lrun[0][:], 0.0), writes=[Lrun[0]])

                slabs = [
                    (0, 512, "qk", (0, 0)),
                    (512, 512, "qk", (8, 1)),
                    (1024, 256, "v", 0),
                    (1280, 384, "qk", (16, 2)),
                    (1664, 384, "qk", (22, 3)),
                    (2048, 390, "vf", 256),
                    (2438, 384, "qk", (28, 4)),
                    (2822, 384, "qk", (34, 5)),
                    (3206, 384, "v", 640),
                ]
                cnt = {"z": 0, "zn": 0, "qk": 0, "v": 0, "s": 0, "qp": 0}
                pend_n = []
                NT1 = int(os.environ.get('K_NT', NT))

                def f_load(t):
                    h_ = hin[t % 3]
                    P.op("sp", lambda e: e.dma_start(out=h_[:], in_=Xl[t * 128:(t + 1) * 128, :]), reads=[Xb], writes=[h_], dma=True)

                def f_norm(t):
                    s2 = t % 2
                    h_, ss_, rr_, hn_ = hin[t % 3], ss[s2], rr[s2], hn[s2]
                    P.op("act", lambda e: e.activation(out=junk[:], in_=h_[:], func=AF.Square, accum_out=ss_[:]), reads=[h_], writes=[junk, ss_])
                    rstd_op(rr_[:], rr_, ss_[:], ss_, 1.0 / D)
                    P.op("dve", lambda e: e.scalar_tensor_tensor(out=hn_[:], in0=h_[:], scalar=rr_[:, 0:1], in1=gb[:], op0=ALU.mult, op1=ALU.mult),
                         reads=[h_, rr_, gb], writes=[hn_])

                def f_pe(t):
                    s2 = t % 2
                    hn_, hnT_, tp_ = hn[s2], hnT[s2], tp_ps[s2]
                    for k in range(8):
                        P.op("pe", lambda e, k=k: e.transpose(out=tp_[:, k * 128:(k + 1) * 128], in_=hn_[:, k * 128:(k + 1) * 128], identity=ident_bf[:]),
                             reads=[hn_, ident_bf], writes=[tp_], signal=(k == 7))
                    P.op("act", lambda e: e.activation(out=hnT_[:].rearrange("p a b -> p (a b)"), in_=tp_[:], func=AF.Copy), reads=[tp_], writes=[hnT_])

                def do_slab(t, c0, ncol, kind, info):
                    s2 = t % 2
                    hnT_ = hnT[s2]
                    zp = z_ps[cnt["z"] % 3]
                    cnt["z"] += 1
                    for k in range(8):
                        P.op("pe", lambda e, k=k: e.matmul(zp[:, 0:ncol], hnT_[:, k, :], wbf[:, k, c0:c0 + ncol], start=(k == 0), stop=(k == 7)),
                             reads=[hnT_, wbf], writes=[zp], signal=(k == 7))
                    while len(pend_n) > 0 and not (kind == "qk" and len(pend_n) == 0):
                        pend_n.pop(0)()
                    if kind == "qk":
                        ht0, gi = info
                        nh = ncol // 64
                        z_ = zs[cnt["s"] % 3]
                        cnt["s"] += 1
                        i2 = cnt["zn"] % 12
                        cnt["zn"] += 1
                        ssh_, rh_, zn_ = ssh[i2], rh[i2], zn[i2]
                        sq_ = zs[cnt["s"] % 3]
                        P.op("act", lambda e: e.activation(out=sq_[:, 0:ncol], in_=zp[:, 0:ncol], func=AF.Square), reads=[zp], writes=[sq_])
                        P.op("dve", lambda e: e.tensor_reduce(out=ssh_[:, 0:nh], in_=sq_[:, 0:ncol].rearrange("p (a b) -> p a b", b=64), axis=AX.X, op=ALU.add),
                             reads=[sq_], writes=[ssh_])

                        def norm_stage():
                            rstd_op(rh_[:, 0:nh], rh_, ssh_[:, 0:nh], ssh_, 1.0 / 64)
                            P.op("dve", lambda e: e.tensor_tensor(out=zn_[:, 0:ncol].rearrange("p (a b) -> p a b", b=64), in0=zp[:, 0:ncol].rearrange("p (a b) -> p a b", b=64),
                                                                  in1=bcast_ap(rh_[:, 0:nh], 64), op=ALU.mult), reads=[zp, rh_], writes=[zn_])
                        pend_n.append(norm_stage)

                        def late():
                            qp_ = qt_ps[cnt["qp"] % 2]
                            cnt["qp"] += 1
                            qk_ = qkt[cnt["qk"] % 8]
                            cnt["qk"] += 1
                            for hh in range(nh):
                                P.op("pe", lambda e, hh=hh: e.transpose(out=qp_[0:64, hh * 128:(hh + 1) * 128], in_=zn_[:, hh * 64:(hh + 1) * 64], identity=ident_bf[:]),
                                     reads=[zn_, ident_bf], writes=[qp_], signal=(hh == nh - 1))
                            P.op("dve", lambda e: e.tensor_scalar_mul(out=qk_[:, 0:nh, :].rearrange("p a b -> p (a b)"), in0=qp_[0:64, 0:nh * 128],
                                                                      scalar1=qg[:, gi:gi + 1]), reads=[qp_, qg], writes=[qk_])
                            if not os.environ.get("K_NOQKST"):
                                P.op(STQ, lambda e: e.dma_start(out=qkT[ht0:ht0 + nh, :, t * 128:(t + 1) * 128].rearrange("h d t -> d h t"), in_=qk_[:, 0:nh, :]),
                                     reads=[qk_], writes=[dram_bufs["qkT"]], dma=True)
                        return late
                    vc0 = info
                    nv = 256 if kind == "v" and ncol == 256 else 384
                    v_ = vsb[cnt["v"] % 6]
                    cnt["v"] += 1
                    P.op("act", lambda e: e.activation(out=v_[:, 0:nv], in_=zp[:, 0:nv], func=AF.Copy), reads=[zp], writes=[v_])
                    for (a0, a1) in ((0, 256), (256, nv)):
                        if a1 > a0:
                            P.op(STQ, lambda e, a0=a0, a1=a1: e.dma_start(out=vS[t * 128:(t + 1) * 128, vc0 + a0:vc0 + a1], in_=v_[:, a0:a1]),
                                 reads=[v_], writes=[dram_bufs["vS"]], dma=True)
                    if kind != "vf":
                        return None
                    t6_, e6_, l6_, fq_ = t6[s2], e6[s2], l6[s2], fq6[s2]
                    Lp, Ln_ = Lrun[t % 2], Lrun[(t + 1) % 2]
                    l8_, L8_, cp_ = l8[s2], L8[s2], cpy[s2]
                    P.op("dve", lambda e: e.tensor_tensor(out=t6_[:], in0=zp[:, 384:390], in1=fb[:], op=ALU.add), reads=[zp, fb], writes=[t6_])
                    P.op("act", lambda e: e.activation(out=e6_[:], in_=t6_[:], func=AF.Exp, scale=-1.0), reads=[t6_], writes=[e6_])
                    P.op("act", lambda e: e.activation(out=l6_[:], in_=e6_[:], func=AF.Ln, bias=1.0), reads=[e6_], writes=[l6_])
                    P.op("dve", lambda e: e.memset(l8_[:], 0.0), writes=[l8_])
                    P.op("dve", lambda e: e.tensor_copy(out=l8_[:, 0:6], in_=l6_[:]), reads=[l6_, l8_], writes=[l8_])
                    P.op("dve", lambda e: e.memset(L8_[:], 0.0), writes=[L8_])
                    P.op("dve", lambda e: e.tensor_copy(out=L8_[:, 0:6], in_=Lp[:]), reads=[Lp, L8_], writes=[L8_])
                    P.op("dve", lambda e: e.tensor_tensor(out=Ln_[:], in0=Lp[:], in1=l6_[:], op=ALU.add), reads=[Lp, l6_], writes=[Ln_])

                    def late():
                        P.op("pe", lambda e: e.matmul(cum_ps[:, 0:8], triU[:], l8_[:], start=True, stop=True), reads=[l8_, triU], writes=[cum_ps])
                        P.op("pe", lambda e: e.matmul(cum_ps[:, 8:16], ones_f[:], L8_[:], start=True, stop=True), reads=[L8_, ones_f], writes=[cum_ps])
                        P.op("dve", lambda e: e.tensor_copy(out=cp_[:], in_=cum_ps[:, 8:16]), reads=[cum_ps], writes=[cp_])
                        P.op("dve", lambda e: e.tensor_tensor(out=Fneg[:, t, :], in0=cum_ps[:, 0:6], in1=cp_[:, 0:6], op=ALU.add), reads=[cum_ps, cp_], writes=[Fneg])
                        P.op("dve", lambda e: e.tensor_scalar_mul(out=fq_[:], in0=Fneg[:, t, :], scalar1=-1.0), reads=[Fneg], writes=[fq_])
                        P.op(STQ, lambda e: e.dma_start(out=bass.AP(FdT.tensor, t * 128, [[1, 128], [S, 6]]), in_=fq_[:], allow_slow_non_contiguous=True),
                             reads=[fq_], writes=[dram_bufs["FdT"]], dma=True)
                    return late

                f_load(0)
                if NT1 > 1:
                    f_load(1)
                f_norm(0)
                f_pe(0)
                prev = []
                for t in range(NT1):
                    if t + 2 < NT1:
                        f_load(t + 2)
                    if t + 1 < NT1:
                        f_norm(t + 1)
                    cur = []
                    for si, (c0, ncol, kind, info) in enumerate(slabs):
                        late = do_slab(t, c0, ncol, kind, info)
                        if late is not None:
                            cur.append(late)
                        if prev:
                            prev.pop(0)()
                        if si == 5 and t + 1 < NT1:
                            f_pe(t + 1)
                    while prev:
                        prev.pop(0)()
                    prev = cur
                while prev:
                    prev.pop(0)()
                P.barrier()
                P.emit()

        def attention(L):
            def run_group(kind):
                with contextlib.ExitStack() as st:
                    def T(name, shape, dt, n=1):
                        r = [Tl(st.enter_context(nc.sbuf_tensor("%s_%d_%d" % (name, _uid(), i), shape, dt)), name) for i in range(n)]
                        return r if n > 1 else r[0]

                    def PT(name, shape, dt, n=1):
                        r = [Tl(st.enter_context(nc.psum_tensor("%s_%d_%d" % (name, _uid(), i), shape, dt)), name) for i in range(n)]
                        for x_ in r:
                            x_.b.psum = True
                        return r if n > 1 else r[0]

                    qkr = T("qkr", [128, S], BF16, 4)
                    Vt = T("Vt", [128, NT, 65], BF16, 4)
                    tmp = T("tmp", [128, 512], F32, 8)
                    PTt = T("PTt", [128, 512], BF16, 6)
                    osb = T("osb", [65, 512], F32, 2)
                    onrm = T("onrm", [128, 4, 64], F32, 6)
                    rec = T("rec", [128, 4], F32, 2)
                    obf = T("obf", [128, 4, 64], BF16, 4)
                    w1 = T("w1", [128, 512], F32)
                    w2 = T("w2", [128, 512], F32)
                    w3 = T("w3", [128, 512], F32)
                    w4 = T("w4", [128, 512], F32)
                    ps_s = PT("ps_s", [128, 512], F32, 4)
                    acc = PT("acc", [128, 512], F32, 2)
                    otp = PT("otp", [128, 512], F32, 1)
                    otp = [otp, otp]
                    ctr = {"qk": 0, "v": 0, "ps": 0, "tmp": 0, "pt": 0, "acc": 0, "osb": 0, "on": 0, "rec": 0, "obf": 0}

                    for v_ in Vt:
                        P.op("dve", lambda e, v_=v_: e.memset(v_[:, :, 64:65], 1.0), writes=[v_])
                    P.op("dve", lambda e: e.memset(w2[:], NEG), writes=[w2])

                    def sel_neg(dst, p0, p1, pattern, base, cm_):
                        if len(pattern) == 2:
                            o_ = dst[p0:p1, :].rearrange("p (a b) -> p a b", b=pattern[1][1])
                            i_ = w2[p0:p1, :].rearrange("p (a b) -> p a b", b=pattern[1][1])
                        else:
                            o_, i_ = dst[p0:p1, :], w2[p0:p1, :]
                        P.op("pool", lambda e: e.affine_select(out=o_, in_=i_, pattern=pattern, compare_op=ALU.is_ge, fill=0.0, base=base,
                                                              channel_multiplier=cm_), reads=[w2], writes=[dst])

                    def flip(dst_ap, dst_tl, src):
                        P.op("dve", lambda e: e.tensor_scalar(out=dst_ap, in0=src[:], scalar1=-1.0, scalar2=NEG, op0=ALU.mult, op1=ALU.add),
                             reads=[src], writes=[dst_tl])

                    def load_pair(htA, htB):
                        tl = qkr[ctr["qk"] % 4]
                        ctr["qk"] += 1
                        P.op("sp", lambda e: e.dma_start(out=tl[0:64, :], in_=qkT[htA, :, :]), reads=[dram_bufs["qkT"]], writes=[tl], dma=True)
                        P.op("sp", lambda e: e.dma_start(out=tl[64:128, :], in_=qkT[htB, :, :]), reads=[dram_bufs["qkT"]], writes=[tl], dma=True)
                        return tl

                    def load_v(vcol):
                        tl = Vt[ctr["v"] % 4]
                        ctr["v"] += 1
                        P.op("sp", lambda e, tl=tl, vcol=vcol: e.dma_start(out=tl[:, :, 0:64], in_=vS[:, vcol:vcol + 64].rearrange("(n p) c -> p n c", p=128)),
                             reads=[dram_bufs["vS"]], writes=[tl], dma=True)
                        return tl

                    S1, S2 = 3, 1
                    jobs = []

                    def attn_pair(Qp, Kp, qt, subs):
                        accs = []
                        for _ in subs:
                            accs.append(acc[ctr["acc"] % 2])
                            ctr["acc"] += 1
                        n = len(subs[0][1])
                        for idx in range(n):
                            sj = []
                            for half, (V, ktiles, cb) in enumerate(subs):
                                kt, b1, b2, ab = ktiles[idx][:4]
                                sj.append(dict(V=V, kt=kt, b1=b1, b2=b2, ab=ab, a_=accs[half], cb=cb, first=(idx == 0), last=(idx == n - 1)))
                            e0 = subs[0][1][idx]
                            cols = e0[4] if len(e0) > 4 else (0, 512)
                            mcols = e0[5] if len(e0) > 5 else cols
                            jobs.append(dict(Qp=Qp, Kp=Kp, qt=qt, kt=e0[0], subs=sj, cols=cols, mcols=mcols))

                    def st_qk_dve(jb):
                        Qp, Kp, qt, kt = jb["Qp"], jb["Kp"], jb["qt"], jb["kt"]
                        (c0, c1), (m0, m1) = jb["cols"], jb["mcols"]
                        pss = []
                        for half in range(2):
                            ps = ps_s[ctr["ps"] % 4]
                            ctr["ps"] += 1
                            pss.append(ps)
                            lo = 64 * half
                            P.op("pe", lambda e, ps=ps, lo=lo: e.matmul(ps[:, c0:c1], Kp[lo:lo + 64, kt * 128:(kt + 1) * 128], Qp[lo:lo + 64, qt * 512 + c0:qt * 512 + c1],
                                                                       start=True, stop=True), reads=[Kp, Qp], writes=[ps])
                        for half in range(2):
                            sb = jb["subs"][half]
                            ps = pss[half]
                            tm = tmp[ctr["tmp"] % 8]
                            ctr["tmp"] += 1
                            sb["tm"] = tm
                            b1, b2 = sb["b1"], sb["b2"]
                            P.op("dve", lambda e, ps=ps, tm=tm, b1=b1: e.tensor_tensor(out=tm[:, c0:c1], in0=ps[:, c0:c1], in1=b1[0][:, c0:c1], op=ALU.add),
                                 reads=[ps, b1[1]], writes=[tm])
                            if b2 is not None:
                                P.op("dve", lambda e, tm=tm, b2=b2: e.tensor_tensor(out=tm[:, m0:m1], in0=tm[:, m0:m1], in1=b2[0][:, m0:m1], op=ALU.add),
                                     reads=[tm, b2[1]], writes=[tm])

                    def st_act(jb):
                        c0, c1 = jb["cols"]
                        for sb in jb["subs"]:
                            tm, ab = sb["tm"], sb["ab"]
                            pt = PTt[ctr["pt"] % 6]
                            ctr["pt"] += 1
                            sb["pt"] = pt
                            if isinstance(ab, tuple):
                                P.op("act", lambda e, tm=tm, pt=pt, ab=ab: e.activation(out=pt[:, c0:c1], in_=tm[:, c0:c1], func=AF.Exp, bias=ab[0]), reads=[tm, ab[1]], writes=[pt])
                            else:
                                P.op("act", lambda e, tm=tm, pt=pt, ab=ab: e.activation(out=pt[:, c0:c1], in_=tm[:, c0:c1], func=AF.Exp, bias=float(ab)), reads=[tm], writes=[pt])

                    def st_pv(jb):
                        c0, c1 = jb["cols"]
                        for sb in jb["subs"]:
                            a_, V, kt, pt, first, last = sb["a_"], sb["V"], sb["kt"], sb["pt"], sb["first"], sb["last"]
                            assert not first or (c0, c1) == (0, 512)
                            P.op("pe", lambda e, a_=a_, V=V, kt=kt, pt=pt, first=first, last=last: e.matmul(a_[0:65, c0:c1], V[:, kt, :], pt[:, c0:c1], start=first, stop=last),
                                 reads=[V, pt], writes=[a_])
                        for sb in jb["subs"]:
                            if sb["last"]:
                                finalize(sb)

                    def finalize(sb):
                        a_ = sb["a_"]
                        o_ = osb[ctr["osb"] % 2]
                        ctr["osb"] += 1
                        op_ = otp[0]
                        on_ = onrm[ctr["on"] % 6]
                        ctr["on"] += 1
                        rc_ = rec[ctr["rec"] % 2]
                        ctr["rec"] += 1
                        P.op("act", lambda e: e.activation(out=o_[:], in_=a_[0:65, :], func=AF.Copy), reads=[a_], writes=[o_])
                        for s_ in range(4):
                            P.op("pe", lambda e, s_=s_: e.transpose(out=op_[:, s_ * 65:(s_ + 1) * 65], in_=o_[:, s_ * 128:(s_ + 1) * 128], identity=ident_f[0:65, 0:65]),
                                 reads=[o_, ident_f], writes=[op_], signal=(s_ == 3))
                        P.op("dve", lambda e: e.reciprocal(out=rc_[:], in_=op_[:, 0:260].rearrange("p (a b) -> p a b", b=65)[:, :, 64]), reads=[op_], writes=[rc_])
                        P.op("dve", lambda e: e.tensor_tensor(out=on_[:], in0=op_[:, 0:260].rearrange("p (a b) -> p a b", b=65)[:, :, 0:64], in1=bcast_ap(rc_[:, 0:4], 64), op=ALU.mult),
                             reads=[op_, rc_], writes=[on_])
                        sb["cb"](on_)

                    def flush():
                        n = len(jobs)
                        for i in range(min(S1, n)):
                            st_qk_dve(jobs[i])
                        for i in range(min(S2, n)):
                            st_act(jobs[i])
                        for i in range(n):
                            if i + S1 < n:
                                st_qk_dve(jobs[i + S1])
                            if i + S2 < n:
                                st_act(jobs[i + S2])
                            st_pv(jobs[i])
                        del jobs[:]

                    def store_o(ob_, qt, col):
                        P.op(STQ, lambda e: e.dma_start(out=att[qt * 512:(qt + 1) * 512, col:col + 64].rearrange("(s p) c -> p s c", p=128), in_=ob_[:]),
                             reads=[ob_], writes=[dram_bufs["att"]], dma=True)

                    if kind == "diff":
                        T0 = T("T0", [128, 4, 512], F32)
                        Bd = T("Bd", [128, 16, 512], F32)
                        lamp = T("lamp", [128, 4, 64], F32)
                        lpr = T("lpr", [128, 2, 64], F32)
                        ls = T("ls", [128, 2], F32)
                        le = T("le", [128, 2], F32)
                        nlam = T("nlam", [128, 1], F32)
                        gsub = T("gsub", [128, 64], F32)
                        od = T("od", [128, 4, 64], F32, 2)
                        osq = T("osq", [128, 4, 64], F32)
                        oss = T("oss", [128, 4], F32, 2)
                        ors = T("ors", [128, 4], F32, 2)
                        lam_init = 0.8 - 0.6 * math.exp(-0.3 * L)
                        P.op("sp", lambda e: e.dma_start(out=lamp[:].rearrange("p a b -> p (a b)"), in_=bass.AP(lam_params.tensor, L * 256, [[0, 128], [1, 256]])),
                             writes=[lamp], dma=True)
                        P.op("dve", lambda e: e.tensor_tensor(out=lpr[:], in0=bass.AP(lamp.t, 0, [[256, 128], [128, 2], [1, 64]]),
                                                              in1=bass.AP(lamp.t, 64, [[256, 128], [128, 2], [1, 64]]), op=ALU.mult), reads=[lamp], writes=[lpr])
                        P.op("dve", lambda e: e.tensor_reduce(out=ls[:], in_=lpr[:], axis=AX.X, op=ALU.add), reads=[lpr], writes=[ls])
                        P.op("act", lambda e: e.activation(out=le[:], in_=ls[:], func=AF.Exp), reads=[ls], writes=[le])
                        P.op("dve", lambda e: e.scalar_tensor_tensor(out=nlam[:], in0=le[:, 1:2], scalar=-lam_init, in1=le[:, 0:1], op0=ALU.add, op1=ALU.subtract),
                             reads=[le], writes=[nlam])
                        P.op("sp", lambda e: e.dma_start(out=gsub[:], in_=bass.AP(subln_gain.tensor, L * 64, [[0, 128], [1, 64]])), writes=[gsub], dma=True)
                        P.op("dve", lambda e: e.tensor_scalar_mul(out=gsub[:], in0=gsub[:], scalar1=(1.0 - lam_init)), reads=[gsub], writes=[gsub])
                        P.op("pool", lambda e: e.iota(w1[:], [[1, 512]], base=0, channel_multiplier=-1, allow_small_or_imprecise_dtypes=True), writes=[w1])
                        for h in range(4):
                            P.op("dve", lambda e, h=h: e.tensor_scalar_mul(out=T0[:, h, :], in0=w1[:], scalar1=-SLOPES[h]), reads=[w1], writes=[T0])
                        for j in range(4):
                            for half in range(2):
                                sel_neg(w3, 64 * half, 64 * half + 64, [[64, 8], [0, 64]], -(128 * j + 64 * half), 0)
                            flip(w3[:], w3, w3)
                            P.op("pool", lambda e, j=j: e.iota(w1[:], [[1, 512]], base=-128 * j, channel_multiplier=-1, allow_small_or_imprecise_dtypes=True), writes=[w1])
                            P.op("dve", lambda e: e.tensor_scalar_mul(out=w4[:], in0=w1[:], scalar1=-1.0), reads=[w1], writes=[w4])
                            P.op("dve", lambda e: e.tensor_tensor(out=w1[:], in0=w1[:], in1=w4[:], op=ALU.max), reads=[w1, w4], writes=[w1])
                            for h in range(4):
                                P.op("dve", lambda e, h=h, j=j: e.scalar_tensor_tensor(out=Bd[:, h * 4 + j, :], in0=w1[:], scalar=-SLOPES[h], in1=w3[:],
                                                                                      op0=ALU.mult, op1=ALU.add), reads=[w1, w3], writes=[Bd])
                        def load_head(h):
                            return (load_pair(h, 4 + h), load_pair(8 + h, 12 + h), load_v(h * 64))

                        def combine(o1, o2, qt, h):
                            i2 = ctr["obf"] % 2
                            ctr["obf"] += 1
                            od_, oss_, ors_, ob_ = od[i2], oss[i2], ors[i2], obf[i2]
                            P.op("dve", lambda e: e.scalar_tensor_tensor(out=od_[:], in0=o2[:], scalar=nlam[:, 0:1], in1=o1[:], op0=ALU.mult, op1=ALU.add),
                                 reads=[o1, o2, nlam], writes=[od_])
                            P.op("dve", lambda e: e.tensor_tensor(out=osq[:], in0=od_[:], in1=od_[:], op=ALU.mult), reads=[od_], writes=[osq])
                            P.op("dve", lambda e: e.tensor_reduce(out=oss_[:], in_=osq[:], axis=AX.X, op=ALU.add), reads=[osq], writes=[oss_])
                            rstd_op(ors_[:], ors_, oss_[:], oss_, 1.0 / 64)
                            P.op("dve", lambda e: e.tensor_tensor(out=od_[:], in0=od_[:], in1=bcast_ap(ors_[:, 0:4], 64), op=ALU.mult), reads=[od_, ors_], writes=[od_])
                            P.op("dve", lambda e: e.tensor_tensor(out=ob_[:], in0=od_[:], in1=bass.AP(gsub.t, 0, [[64, 128], [0, 4], [1, 64]]), op=ALU.mult),
                                 reads=[od_, gsub], writes=[ob_])
                            store_o(ob_, qt, h * 64)

                        nxt = load_head(0)
                        for h in range(4):
                            Qp, Kp, V = nxt
                            if h + 1 < 4:
                                nxt = load_head(h + 1)
                            for qt in range(8):
                                box = []

                                def fin(on_, box=box, qt=qt, h=h):
                                    box.append(on_)
                                    if len(box) == 2:
                                        combine(box[0], box[1], qt, h)
                                kts = []
                                for kt in range(4 * qt + 4):
                                    if kt < 4 * qt:
                                        kts.append((kt, (T0[:, h, :], T0), None, -SLOPES[h] * (512 * qt - 128 * kt)))
                                    else:
                                        kts.append((kt, (Bd[:, h * 4 + (kt - 4 * qt), :], Bd), None, 0.0, (128 * (kt - 4 * qt), 512)))
                                attn_pair(Qp, Kp, qt, [(V, kts, fin), (V, kts, fin)])
                            flush()

                    elif kind == "fox":
                        FQb = T("FQb", [128, S], F32, 4)
                        caus = T("caus", [128, 4, 512], F32)
                        for j in range(4):
                            sel_neg(w3, 0, 128, [[1, 512]], -128 * j, -1)
                            flip(caus[:, j, :], caus, w3)
                        def cast_store(col):
                            def fin(on_, qt):
                                ob_ = obf[ctr["obf"] % 2]
                                ctr["obf"] += 1
                                P.op("dve", lambda e: e.tensor_copy(out=ob_[:], in_=on_[:]), reads=[on_], writes=[ob_])
                                store_o(ob_, qt, col)
                            return fin

                        fqc = [0]

                        def load_head(h0):
                            Qp, Kp = load_pair(16 + h0, 17 + h0), load_pair(22 + h0, 23 + h0)
                            out_ = [Qp, Kp]
                            for h in (h0, h0 + 1):
                                V = load_v(256 + h * 64)
                                fq = FQb[fqc[0] % 4]
                                fqc[0] += 1
                                P.op("sp", lambda e, fq=fq, h=h: e.dma_start(out=fq[:], in_=bass.AP(FdT.tensor, h * S, [[0, 128], [1, S]])),
                                     reads=[dram_bufs["FdT"]], writes=[fq], dma=True)
                                out_.append((V, fq))
                            return out_

                        nxt = load_head(0)
                        for h0 in (0, 2, 4):
                            Qp, Kp, hA, hB = nxt
                            if h0 + 2 < 6:
                                nxt = load_head(h0 + 2)
                            for qt in range(8):
                                subs = []
                                for h, (V, fq) in ((h0, hA), (h0 + 1, hB)):
                                    fin_h = cast_store(256 + h * 64)
                                    kts = []
                                    for kt in range(4 * qt + 4):
                                        if kt < 4 * qt:
                                            kts.append((kt, (fq[:, qt * 512:(qt + 1) * 512], fq), None, (Fneg[:, kt, h:h + 1], Fneg)))
                                        else:
                                            j_ = kt - 4 * qt
                                            kts.append((kt, (fq[:, qt * 512:(qt + 1) * 512], fq), (caus[:, j_, :], caus), (Fneg[:, kt, h:h + 1], Fneg),
                                                        (128 * j_, 512), (128 * j_, 128 * j_ + 128)))
                                    subs.append((V, kts, lambda on_, qt=qt, fin_h=fin_h: fin_h(on_, qt)))
                                attn_pair(Qp, Kp, qt, subs)
                            flush()

                    else:
                        cm = T("cm", [128, 8, 512], F32)
                        Bc = T("Bc", [128, 8, 512], F32, 4)
                        Tp = T("Tp", [128, 512], F32, 2)
                        jps = otp[0]
                        eb = dram_bufs["ext"]
                        exs = T("exs", [6, 1536], F32)
                        P.op("sp", lambda e: e.dma_start(out=exs[:, 383:640], in_=rel_bias[L, :, :]), writes=[exs], dma=True)
                        P.op("dve", lambda e: e.tensor_copy(out=exs[:, 0:383], in_=bass.AP(exs.t, 383, [[1536, 6], [0, 383]])), reads=[exs], writes=[exs])
                        P.op("dve", lambda e: e.tensor_copy(out=exs[:, 640:1536], in_=bass.AP(exs.t, 639, [[1536, 6], [0, 896]])), reads=[exs], writes=[exs])
                        P.op("sp", lambda e: e.dma_start(out=ext[:, :], in_=exs[:]), reads=[exs], writes=[eb], dma=True)
                        for j in range(8):
                            for half in range(2):
                                kc = 128 * j + 64 * half
                                sel_neg(w3, 64 * half, 64 * half + 64, [[64, 8], [0, 64]], 512 - kc, 0)
                                sel_neg(w1, 64 * half, 64 * half + 64, [[64, 8], [0, 64]], -kc - 64, 0)
                            flip(w3[:], w3, w3)
                            P.op("dve", lambda e, j=j: e.tensor_tensor(out=cm[:, j, :], in0=w3[:], in1=w1[:], op=ALU.add), reads=[w3, w1], writes=[cm])
                        tpc = [0]

                        def cast_store(col):
                            def fin(on_, qt):
                                ob_ = obf[ctr["obf"] % 2]
                                ctr["obf"] += 1
                                P.op("dve", lambda e: e.tensor_copy(out=ob_[:], in_=on_[:]), reads=[on_], writes=[ob_])
                                store_o(ob_, qt, col)
                            return fin

                        bcc = [0]

                        def load_head(h0):
                            Qp, Kp = load_pair(28 + h0, 29 + h0), load_pair(34 + h0, 35 + h0)
                            out_ = [Qp, Kp]
                            for h in (h0, h0 + 1):
                                V = load_v(640 + h * 64)
                                bc = Bc[bcc[0] % 4]
                                bcc[0] += 1
                                for j in range(8):
                                    tp_ = Tp[tpc[0] % 2]
                                    tpc[0] += 1
                                    P.op("sp", lambda e, tp_=tp_, j=j, h=h: e.dma_start(out=tp_[:], in_=bass.AP(ext.tensor, h * 1536 + 896 - 128 * j, [[1, 128], [1, 512]])),
                                         reads=[eb], writes=[tp_], dma=True)
                                    P.op("pe", lambda e, tp_=tp_: e.matmul(jps[:], J_f[:], tp_[:], start=True, stop=True), reads=[tp_, J_f], writes=[jps])
                                    P.op("dve", lambda e, j=j, bc=bc: e.tensor_tensor(out=bc[:, j, :], in0=jps[:], in1=cm[:, j, :], op=ALU.add), reads=[jps, cm], writes=[bc])
                                out_.append((V, bc))
                            return out_

                        nxt = load_head(0)
                        for h0 in (0, 2, 4):
                            Qp, Kp, hA, hB = nxt
                            if h0 + 2 < 6:
                                nxt = load_head(h0 + 2)
                            for qt in range(8):
                                subs = []
                                for h, (V, bc) in ((h0, hA), (h0 + 1, hB)):
                                    fin_h = cast_store(640 + h * 64)
                                    kts = []
                                    for j in (3, 4, 0, 1, 2, 5, 6, 7):
                                        kt = 4 * qt - 4 + j
                                        if kt < 0:
                                            continue
                                        kts.append((kt, (bc[:, j, :], bc), None, 0.0, (0, 128 * (j + 1)) if j <= 3 else (128 * (j - 4), 512)))
                                    subs.append((V, kts, lambda on_, qt=qt, fin_h=fin_h: fin_h(on_, qt)))
                                attn_pair(Qp, Kp, qt, subs)
                            flush()
                    P.barrier()
                    P.emit()

            for kind in ("diff", "fox", "chunk"):
                run_group(kind)

        def load_weight_bf(st, name, src_rows, nrow_chunks, ncols, col0=0, row0=0, stg=None):
            wt = Tl(st.enter_context(nc.sbuf_tensor("%s_%d" % (name, _uid()), [128, nrow_chunks, ncols], BF16)), name)
            nsub = (ncols + 2815) // 2816
            if stg is None:
                stg = [Tl(st.enter_context(nc.sbuf_tensor("%s_stg%d_%d" % (name, i, _uid()), [128, min(ncols, 2816)], F32)), name + "stg") for i in range(3)]
            c = 0
            for k in range(nrow_chunks):
                for sb in range(nsub):
                    a = sb * 2816
                    w = min(2816, ncols - a)
                    sg = stg[c % len(stg)]
                    c += 1
                    P.op("sp", lambda e, sg=sg, k=k, a=a, w=w: e.dma_start(out=sg[:, 0:w], in_=src_rows[row0 + k * 128:row0 + (k + 1) * 128, col0 + a:col0 + a + w]),
                         writes=[sg], dma=True)
                    P.cast(wt[:, k, a:a + w], wt, sg[:, 0:w], sg)
            return wt

        def norm_T(T_, PT_, tiles, h_, gbt, idx):
            s2 = idx % 2
            ss_, rr_, hn_, tp_ = tiles["ss"][s2], tiles["rr"][s2], tiles["hn"][s2], tiles["tp"][s2]
            junk = tiles["junk"]
            P.op("act", lambda e: e.activation(out=junk[:], in_=h_[:], func=AF.Square, accum_out=ss_[:]), reads=[h_], writes=[junk, ss_])
            rstd_op(rr_[:], rr_, ss_[:], ss_, 1.0 / D)
            P.op("dve", lambda e: e.scalar_tensor_tensor(out=hn_[:], in0=h_[:], scalar=rr_[:, 0:1], in1=gbt[:], op0=ALU.mult, op1=ALU.mult),
                 reads=[h_, rr_, gbt], writes=[hn_])
            for k in range(8):
                P.op("pe", lambda e, k=k: e.transpose(out=tp_[:, k * 128:(k + 1) * 128], in_=hn_[:, k * 128:(k + 1) * 128], identity=ident_bf[:]),
                     reads=[hn_, ident_bf], writes=[tp_], signal=(k == 7))
            return tp_

        def norm_tiles(T, PT):
            return dict(ss=T("ss", [128, 1], F32, 2), rr=T("rr", [128, 1], F32, 2), hn=T("hn", [128, D], BF16, 2),
                        hnT=None, tp=PT("tp", [128, 1024], BF16, 2), junk=T("junk", [128, D], BF16))

        def load_gain(T, name, src, L):
            g = T(name, [128, D], F32)
            P.op("sp", lambda e: e.dma_start(out=g[:], in_=bass.AP(src.tensor, L * D, [[0, 128], [1, D]])), writes=[g], dma=True)
            return g

        def mk_alloc(st):
            def T(name, shape, dt, n=1):
                r = [Tl(st.enter_context(nc.sbuf_tensor("%s_%d_%d" % (name, _uid(), i), shape, dt)), name) for i in range(n)]
                return r if n > 1 else r[0]

            def PT(name, shape, dt, n=1):
                r = [Tl(st.enter_context(nc.psum_tensor("%s_%d_%d" % (name, _uid(), i), shape, dt)), name) for i in range(n)]
                for x_ in r:
                    x_.b.psum = True
                return r if n > 1 else r[0]
            return T, PT

        def phase3(L, Xl, Xb, Yd, Yb):
            with contextlib.ExitStack() as st:
                T, PT = mk_alloc(st)
                wo = load_weight_bf(st, "wo", w_out[L], 8, D)
                a_in = T("a_in", [128, D], BF16, 2)
                h_in = T("h_in", [128, D], F32, 2)
                aT = T("aT", [128, 8, 128], BF16, 2)
                h_o = T("h_o", [128, D], F32, 2)
                tp = PT("tp", [128, 1024], BF16, 2)
                ops_ = PT("ops", [128, 512], F32, 4)

                def front(t):
                    s2 = t % 2
                    P.op("sp", lambda e: e.dma_start(out=a_in[s2][:], in_=att[t * 128:(t + 1) * 128, :]), reads=[dram_bufs["att"]], writes=[a_in[s2]], dma=True)
                    P.op("sp", lambda e: e.dma_start(out=h_in[s2][:], in_=Xl[t * 128:(t + 1) * 128, :]), reads=[Xb], writes=[h_in[s2]], dma=True)
                    for k in range(8):
                        P.op("pe", lambda e, k=k: e.transpose(out=tp[s2][:, k * 128:(k + 1) * 128], in_=a_in[s2][:, k * 128:(k + 1) * 128], identity=ident_bf[:]),
                             reads=[a_in[s2], ident_bf], writes=[tp[s2]], signal=(k == 7))
                    P.op("act", lambda e: e.activation(out=aT[s2][:].rearrange("p a b -> p (a b)"), in_=tp[s2][:], func=AF.Copy), reads=[tp[s2]], writes=[aT[s2]])

                def back(t, mid=None):
                    s2 = t % 2
                    for n in range(2):
                        if n == 1 and mid is not None:
                            mid()
                        ps = ops_[(2 * t + n) % 4]
                        for k in range(8):
                            P.op("pe", lambda e, k=k, n=n, ps=ps: e.matmul(ps[:], aT[s2][:, k, :], wo[:, k, n * 512:(n + 1) * 512], start=(k == 0), stop=(k == 7)),
                                 reads=[aT[s2], wo], writes=[ps], signal=(k == 7))
                        P.op("dve", lambda e, n=n, ps=ps: e.tensor_tensor(out=h_o[s2][:, n * 512:(n + 1) * 512], in0=ps[:], in1=h_in[s2][:, n * 512:(n + 1) * 512], op=ALU.add),
                             reads=[ps, h_in[s2]], writes=[h_o[s2]])
                    P.op(STQ, lambda e: e.dma_start(out=Yd[t * 128:(t + 1) * 128, :], in_=h_o[s2][:]), reads=[h_o[s2]], writes=[Yb], dma=True)

                front(0)
                for t in range(NT):
                    back(t, (lambda t=t: front(t + 1)) if t + 1 < NT else None)
                P.barrier()
                P.emit()

        def phase4(L, hf, Yd, Yb, Ad, Ab, Od, Ob):
            with contextlib.ExitStack() as st:
                T, PT = mk_alloc(st)
                NCH = 11
                gcol0 = hf * NCH * 128
                vcol0 = DFF + hf * NCH * 128
                stg4 = T("stg4", [128, NCH * 128], F32, 3)
                wug = load_weight_bf(st, "wug", w_up[L], 8, NCH * 128, col0=gcol0, stg=stg4)
                wuv = load_weight_bf(st, "wuv", w_up[L], 8, NCH * 128, col0=vcol0, stg=stg4)
                wd = load_weight_bf(st, "wd", w_down[L], NCH, D, row0=hf * NCH * 128, stg=stg4)
                gbt = load_gain(T, "gbt", ln_ffn, L)
                nt = norm_tiles(T, PT)
                cw = T("cw", [128, 3, 2 * NCH], F32)
                cb = T("cb", [128, 2 * NCH], F32)
                for j in range(3):
                    for (g0, base) in ((0, gcol0), (NCH, vcol0)):
                        P.op("sp", lambda e, j=j, g0=g0, base=base: e.dma_start(
                            out=cw[:, j, g0:g0 + NCH], in_=bass.AP(conv_w.tensor, (L * 3 + j) * 2 * DFF + base, [[1, 128], [128, NCH]]),
                            allow_slow_non_contiguous=True), writes=[cw], dma=True)
                for (g0, base) in ((0, gcol0), (NCH, vcol0)):
                    P.op("sp", lambda e, g0=g0, base=base: e.dma_start(out=cb[:, g0:g0 + NCH], in_=bass.AP(conv_b.tensor, L * 2 * DFF + base, [[1, 128], [128, NCH]]),
                                                                       allow_slow_non_contiguous=True), writes=[cb], dma=True)
                h_in = T("h_in", [128, D], F32, 2)
                a_in = T("a_in", [128, 4, D], F32)
                hnT = T("hnT", [128, 8, 512], BF16, 2)
                ub = T("ub", [128, 514], F32, 3)
                halo = T("halo", [128, 2 * NCH, 2], F32)
                c0 = T("c0", [128, 512], F32, 2)
                c1 = T("c1", [128, 512], F32, 2)
                sg_ = T("sg", [128, 512], F32, 2)
                gT = T("gT", [128, NCH, 512], BF16, 2)
                h_o = T("h_o", [128, D], F32, 2)
                u_ps = PT("u_ps", [128, 512], F32, 3)
                d_ps = PT("d_ps", [128, 512], F32, 3)
                P.op("dve", lambda e: e.memset(halo[:], 0.0), writes=[halo])
                ctr = {"u": 0, "d": 0}
                NT4 = S // 512

                def front(T4):
                    hT = hnT[T4 % 2]
                    for s in range(4):
                        t = T4 * 4 + s
                        hi = h_in[t % 2]
                        P.op("sp", lambda e, t=t, hi=hi: e.dma_start(out=hi[:], in_=Yd[t * 128:(t + 1) * 128, :]), reads=[Yb], writes=[hi], dma=True)
                        tp_ = norm_T(T, PT, nt, hi, gbt, t)
                        P.op("act", lambda e, s=s, tp_=tp_: e.activation(out=hT[:, :, s * 128:(s + 1) * 128], in_=tp_[:].rearrange("p (a b) -> p a b", a=8), func=AF.Copy),
                             reads=[tp_], writes=[hT])

                def back(T4, mid=None):
                    b2 = T4 % 2
                    hT = hnT[b2]
                    g_ = gT[b2]
                    P.op("sp", lambda e: e.dma_start(out=a_in[:], in_=Ad[T4 * 512:(T4 + 1) * 512, :].rearrange("(s p) c -> p s c", p=128)),
                         reads=[Ab], writes=[a_in], dma=True)
                    for c in range(NCH):
                        if c == 4 and mid is not None:
                            mid()
                        res = []
                        for (wt, ci) in ((wug, c), (wuv, NCH + c)):
                            ps = u_ps[ctr["u"] % 3]
                            u_ = ub[ctr["u"] % 3]
                            ctr["u"] += 1
                            for k in range(8):
                                P.op("pe", lambda e, k=k, ps=ps, wt=wt, c=c: e.matmul(ps[:], wt[:, k, c * 128:(c + 1) * 128], hT[:, k, :], start=(k == 0), stop=(k == 7)),
                                     reads=[hT, wt], writes=[ps], signal=(k == 7))
                            a0, a1 = c0[ci // NCH], c1[ci // NCH]
                            P.op("act", lambda e, u_=u_, ci=ci: e.activation(out=u_[:, 0:2], in_=halo[:, ci, :], func=AF.Copy), reads=[halo], writes=[u_])
                            P.op("act", lambda e, ps=ps, u_=u_: e.activation(out=u_[:, 2:514], in_=ps[:], func=AF.Copy), reads=[ps], writes=[u_])
                            P.op("act", lambda e, ps=ps, ci=ci: e.activation(out=halo[:, ci, :], in_=ps[:, 510:512], func=AF.Copy), reads=[ps], writes=[halo])
                            P.op("act", lambda e, ps=ps, a0=a0, ci=ci: e.activation(out=a0[:], in_=ps[:], func=AF.Copy, scale=cw[:, 2, ci:ci + 1]),
                                 reads=[ps, cw], writes=[a0])
                            P.op("dve", lambda e, u_=u_, a0=a0, a1=a1, ci=ci: e.scalar_tensor_tensor(out=a1[:], in0=u_[:, 1:513], scalar=cw[:, 1, ci:ci + 1], in1=a0[:],
                                                                                                op0=ALU.mult, op1=ALU.add), reads=[u_, cw, a0], writes=[a1])
                            P.op("dve", lambda e, u_=u_, a0=a0, a1=a1, ci=ci: e.scalar_tensor_tensor(out=a0[:], in0=u_[:, 0:512], scalar=cw[:, 0, ci:ci + 1], in1=a1[:],
                                                                                                op0=ALU.mult, op1=ALU.add), reads=[u_, cw, a1], writes=[a0])
                            res.append(a0)
                        gt_, vl_ = res
                        sgt = sg_[c % 2]
                        P.op("act", lambda e, gt_=gt_, sgt=sgt, c=c: e.activation(out=sgt[:], in_=gt_[:], func=AF.Silu, bias=cb[:, c:c + 1]), reads=[gt_, cb], writes=[sgt])
                        P.op("dve", lambda e, sgt=sgt, vl_=vl_, c=c: e.scalar_tensor_tensor(out=g_[:, c, :], in0=vl_[:], scalar=cb[:, NCH + c:NCH + c + 1], in1=sgt[:],
                                                                                           op0=ALU.add, op1=ALU.mult), reads=[sgt, vl_, cb], writes=[g_])
                    for s in range(4):
                        t = T4 * 4 + s
                        ho = h_o[t % 2]
                        for n in range(2):
                            ps = d_ps[ctr["d"] % 3]
                            ctr["d"] += 1
                            for c in range(NCH):
                                P.op("pe", lambda e, c=c, n=n, s=s, ps=ps: e.matmul(ps[:], g_[:, c, s * 128:(s + 1) * 128], wd[:, c, n * 512:(n + 1) * 512],
                                                                                  start=(c == 0), stop=(c == NCH - 1)),
                                     reads=[g_, wd], writes=[ps], signal=(c == NCH - 1))
                            P.op("dve", lambda e, n=n, s=s, ps=ps, ho=ho: e.tensor_tensor(out=ho[:, n * 512:(n + 1) * 512], in0=ps[:], in1=a_in[:, s, n * 512:(n + 1) * 512], op=ALU.add),
                                 reads=[ps, a_in], writes=[ho])
                        P.op(STQ, lambda e, t=t, ho=ho: e.dma_start(out=Od[t * 128:(t + 1) * 128, :], in_=ho[:]), reads=[ho], writes=[Ob], dma=True)

                front(0)
                for T4 in range(NT4):
                    back(T4, (lambda T4=T4: front(T4 + 1)) if T4 + 1 < NT4 else None)
                P.barrier()
                P.emit()

        def phase5(L, Wd, Wb, Od, Ob):
            with contextlib.ExitStack() as st:
                T, PT = mk_alloc(st)
                stg5 = T("stg5", [128, D], F32, 3)
                wg = load_weight_bf(st, "wg", w_ple_gate[L], 8, D, stg=stg5)
                wp = load_weight_bf(st, "wp", w_ple_proj[L], 2, D, stg=stg5)
                gbt = load_gain(T, "gbt", ln_ple, L)
                nt = norm_tiles(T, PT)
                h_in = T("h_in", [128, D], F32, 2)
                p_in = T("p_in", [128, PLE], F32, 2)
                p_bf = T("p_bf", [128, PLE], BF16, 2)
                hnT = T("hnT", [128, 8, 128], BF16, 2)
                pT = T("pT", [128, 2, 128], BF16, 2)
                gate = T("gate", [128, D], F32, 2)
                pg = T("pg", [128, D], F32, 2)
                h_o = T("h_o", [128, D], F32, 2)
                ptp = PT("ptp", [128, 1024], BF16, 1)
                g_ps = PT("g_ps", [128, 512], F32, 2)
                p_ps = PT("p_ps", [128, 512], F32, 2)

                def front(t):
                    s2 = t % 2
                    hi, pi, pb, hT, pT_ = h_in[s2], p_in[s2], p_bf[s2], hnT[s2], pT[s2]
                    P.op("sp", lambda e: e.dma_start(out=hi[:], in_=Wd[t * 128:(t + 1) * 128, :]), reads=[Wb], writes=[hi], dma=True)
                    P.op("sp", lambda e: e.dma_start(out=pi[:], in_=pin[L, t * 128:(t + 1) * 128, :]), writes=[pi], dma=True)
                    tp_ = norm_T(T, PT, nt, hi, gbt, t)
                    P.op("act", lambda e: e.activation(out=hT[:].rearrange("p a b -> p (a b)"), in_=tp_[:], func=AF.Copy), reads=[tp_], writes=[hT])
                    P.op("pool", lambda e: e.tensor_copy(out=pb[:], in_=pi[:]), reads=[pi], writes=[pb])
                    for k in range(2):
                        P.op("pe", lambda e, k=k: e.transpose(out=ptp[:, k * 128:(k + 1) * 128], in_=pb[:, k * 128:(k + 1) * 128], identity=ident_bf[:]),
                             reads=[pb, ident_bf], writes=[ptp], signal=(k == 1))
                    P.op("act", lambda e: e.activation(out=pT_[:].rearrange("p a b -> p (a b)"), in_=ptp[:, 0:256], func=AF.Copy), reads=[ptp], writes=[pT_])

                def back(t, mid=None):
                    s2 = t % 2
                    hi, hT, pT_, gt, pg_, ho = h_in[s2], hnT[s2], pT[s2], gate[s2], pg[s2], h_o[s2]
                    for n in range(2):
                        if n == 1 and mid is not None:
                            mid()
                        gp, pp = g_ps[n], p_ps[n]
                        for k in range(8):
                            P.op("pe", lambda e, k=k, n=n, gp=gp: e.matmul(gp[:], hT[:, k, :], wg[:, k, n * 512:(n + 1) * 512], start=(k == 0), stop=(k == 7)),
                                 reads=[hT, wg], writes=[gp], signal=(k == 7))
                        P.op("act", lambda e, n=n, gp=gp: e.activation(out=gt[:, n * 512:(n + 1) * 512], in_=gp[:], func=AF.Sigmoid), reads=[gp], writes=[gt])
                        for k in range(2):
                            P.op("pe", lambda e, k=k, n=n, pp=pp: e.matmul(pp[:], pT_[:, k, :], wp[:, k, n * 512:(n + 1) * 512], start=(k == 0), stop=(k == 1)),
                                 reads=[pT_, wp], writes=[pp], signal=(k == 1))
                        P.op("dve", lambda e, n=n, pp=pp: e.tensor_tensor(out=pg_[:, n * 512:(n + 1) * 512], in0=pp[:], in1=gt[:, n * 512:(n + 1) * 512], op=ALU.mult),
                             reads=[pp, gt], writes=[pg_])
                        P.op("pool", lambda e, n=n: e.tensor_tensor(out=ho[:, n * 512:(n + 1) * 512], in0=pg_[:, n * 512:(n + 1) * 512],
                                                                   in1=hi[:, n * 512:(n + 1) * 512], op=ALU.add), reads=[pg_, hi], writes=[ho])
                    P.op(STQ, lambda e: e.dma_start(out=Od[t * 128:(t + 1) * 128, :], in_=ho[:]), reads=[ho], writes=[Ob], dma=True)

                front(0)
                for t in range(NT):
                    back(t, (lambda t=t: front(t + 1)) if t + 1 < NT else None)
                P.barrier()
                P.emit()

        db = dram_bufs
        for L in range(depth):
            Xl, Xb = (x, db["x"]) if L == 0 else (XN, db["XN"])
            last = (L == depth - 1)
            Od, Ob = (out, db["out"]) if last else (XN, db["XN"])
            if phases is None or 1 in phases:
                phase1(L, Xl, Xb)
            if phases is None or 2 in phases:
                attention(L)
            if phases is None or 3 in phases:
                phase3(L, Xl, Xb, Y, db["Y"])
            if phases is None or 4 in phases:
                phase4(L, 0, Y, db["Y"], Y, db["Y"], Z, db["Z"])
                phase4(L, 1, Y, db["Y"], Z, db["Z"], W, db["W"])
            if phases is None or 5 in phases:
                phase5(L, W, db["W"], Od, Ob)
    return nc


_CACHE = {}


def kernel(**inputs):
    names = ["ln_mix", "w_in", "qk_gain", "lam_params", "subln_gain", "fgate_bias", "rel_bias", "w_out", "ln_ffn", "w_up",
             "conv_w", "conv_b", "w_down", "ln_ple", "w_ple_gate", "w_ple_proj"]
    x = np.ascontiguousarray(np.asarray(inputs["x"], dtype=np.float32))
    p = np.asarray(inputs["p"], dtype=np.float32)
    shared = {n: np.ascontiguousarray(np.asarray(inputs[n], dtype=np.float32)) for n in names}
    if "nc" not in _CACHE:
        _CACHE["nc"] = build_program()
    nc = _CACHE["nc"]
    in_maps = []
    for b in range(8):
        m = {"x": x[b], "p": np.ascontiguousarray(p[:, b])}
        m.update(shared)
        in_maps.append(m)
    res = run_bass_kernel_spmd(nc, in_maps, core_ids=list(range(8)))
    return np.stack([np.asarray(r["out"], dtype=np.float32) for r in res.results], axis=0)
```
